# Optimizing a Trainium2 kernel written in Bass

```python
import math
import jax, jax.numpy as jnp
from jax import lax
import numpy as np

D_MODEL = 2048
BATCH = 2
SEQ = 4096
DEPTH = 2
DEC_BATCH = 32
DEC_SEQ = 64
PAST_LEN = 4096

CHUNK = 64
N_META = 16
MIX_WIDTH = D_MODEL
SB_WIDTH = MIX_WIDTH // 2
SB_HEAD_DIM = 128
N_SB_HEADS = SB_WIDTH // SB_HEAD_DIM
POOL_WIDTH = MIX_WIDTH - SB_WIDTH
POOL_WINDOWS = (2, 4, 8, 16)
N_POOL_GROUPS = len(POOL_WINDOWS)
POOL_GROUP_DIM = POOL_WIDTH // N_POOL_GROUPS
POOL_HIST = max(POOL_WINDOWS) - 1
IN_WIDTH = 3 * SB_WIDTH + POOL_WIDTH
D_FF = ((8 * D_MODEL // 3 + 255) // 256) * 256
BLOCK_Q = 128
RMS_EPS = 1e-6

kernel_name = "hybrid_stickbreak_pool_stream_step"


def rms_norm(x, g):
    xf = x.astype(jnp.float32)
    y = xf * lax.rsqrt(jnp.mean(xf * xf, axis=-1, keepdims=True) + RMS_EPS) * g.astype(jnp.float32)
    return y.astype(x.dtype)


def swiglu(x, wg, wu, wd):
    return (jax.nn.silu(x @ wg) * (x @ wu)) @ wd


def sb_block(q, k, v, q_pos, k_pos):
    z = jnp.einsum('bqhd,bkhd->bhqk', q, k, preferred_element_type=jnp.float32) * (SB_HEAD_DIM ** -0.5)
    mask = (k_pos[None, :] < q_pos[:, None])[None, None]
    log_stay = jnp.where(mask, jax.nn.log_sigmoid(-z), 0.0)
    after = lax.cumsum(log_stay, axis=3, reverse=True) - log_stay
    a = jnp.where(mask, jnp.exp(jax.nn.log_sigmoid(z) + after), 0.0)
    return jnp.einsum('bhqk,bkhd->bqhd', a.astype(v.dtype), v)


def sb_attention(q, k, v, q_pos0):
    B, Tq, H, Dh = q.shape
    k_pos = jnp.arange(k.shape[1])
    if Tq <= BLOCK_Q:
        return sb_block(q, k, v, q_pos0 + jnp.arange(Tq), k_pos)
    nb = -(-Tq // BLOCK_Q)
    pad = nb * BLOCK_Q - Tq
    qp = jnp.pad(q, ((0, 0), (0, pad), (0, 0), (0, 0)))
    qb = qp.reshape(B, nb, BLOCK_Q, H, Dh).transpose(1, 0, 2, 3, 4)
    pb = (q_pos0 + jnp.arange(nb * BLOCK_Q)).reshape(nb, BLOCK_Q)
    ob = lax.map(lambda a: sb_block(a[0], k, v, a[1], k_pos), (qb, pb))
    return ob.transpose(1, 0, 2, 3, 4).reshape(B, nb * BLOCK_Q, H, Dh)[:, :Tq]


def pool_mix(u, hist, n_hist_valid, w_pool, pool_scale):
    B, L, C = u.shape
    P = POOL_HIST
    ext = jnp.concatenate([hist, u], axis=1)
    extf = ext.astype(jnp.float32)
    cs = jnp.concatenate([jnp.zeros((B, 1, C), jnp.float32), jnp.cumsum(extf, axis=1)], axis=1)
    cur = extf[:, P:]
    n_before = (n_hist_valid + jnp.arange(L)).astype(jnp.float32)
    outs = []
    for g, w in enumerate(POOL_WINDOWS):
        sl = slice(g * POOL_GROUP_DIM, (g + 1) * POOL_GROUP_DIM)
        s = cs[:, P + 1:P + 1 + L, sl] - cs[:, P + 1 - w:P + 1 - w + L, sl]
        cnt = jnp.minimum(float(w), n_before + 1.0)[None, :, None]
        outs.append(s / cnt - cur[:, :, sl])
    d = jnp.stack(outs, axis=2).astype(u.dtype)
    y = jnp.einsum('blgc,gcd->blgd', d, w_pool).reshape(B, L, C) * pool_scale
    return y, ext[:, -P:]


def token_mixer(hn, w_in, w_out, w_pool, pool_scale, k_past, v_past, hist, n_hist_valid):
    B, L, _ = hn.shape
    proj = hn @ w_in
    q, k, v, u = jnp.split(proj, [SB_WIDTH, 2 * SB_WIDTH, 3 * SB_WIDTH], axis=-1)
    q = q.reshape(B, L, N_SB_HEADS, SB_HEAD_DIM)
    k = k.reshape(B, L, N_SB_HEADS, SB_HEAD_DIM)
    v = v.reshape(B, L, N_SB_HEADS, SB_HEAD_DIM)
    if k_past is None:
        k_all, v_all, q_pos0 = k, v, 0
    else:
        k_all = jnp.concatenate([k_past.astype(k.dtype), k], axis=1)
        v_all = jnp.concatenate([v_past.astype(v.dtype), v], axis=1)
        q_pos0 = k_past.shape[1]
    o_sb = sb_attention(q, k_all, v_all, q_pos0).reshape(B, L, SB_WIDTH)
    o_pool, new_hist = pool_mix(u, hist.astype(u.dtype), n_hist_valid, w_pool, pool_scale)
    out = jnp.concatenate([o_sb, o_pool], axis=-1) @ w_out
    return out, k, v, new_hist


def trunk_layer(x, w_in, w_out, w_pool, pool_scale, gains,
                f1g, f1u, f1d, f2g, f2u, f2d, k_past, v_past, hist, n_hist_valid):
    h = x + 0.5 * rms_norm(swiglu(rms_norm(x, gains[0]), f1g, f1u, f1d), gains[1])
    m, k, v, new_hist = token_mixer(rms_norm(h, gains[2]), w_in, w_out, w_pool, pool_scale,
                                    k_past, v_past, hist, n_hist_valid)
    h = h + rms_norm(m, gains[3])
    h = h + 0.5 * rms_norm(swiglu(rms_norm(h, gains[4]), f2g, f2u, f2d), gains[5])
    return h, k, v, new_hist


def setup_inputs(seed: int = 0) -> dict:
    key = jax.random.key(seed)
    ks = jax.random.split(key, 20)
    f32 = jnp.float32
    nrm = lambda k, s, sc: jax.random.normal(k, s, f32) * sc
    return {
        "x_prompt": nrm(ks[0], (BATCH, SEQ, D_MODEL), 1.0),
        "x_sample": nrm(ks[1], (DEC_BATCH, DEC_SEQ, D_MODEL), 1.0),
        "cache_k": nrm(ks[2], (DEPTH, DEC_BATCH, PAST_LEN, N_SB_HEADS, SB_HEAD_DIM), 1.0),
        "cache_v": nrm(ks[3], (DEPTH, DEC_BATCH, PAST_LEN, N_SB_HEADS, SB_HEAD_DIM), 1.0),
        "state_pool": nrm(ks[4], (DEPTH, DEC_BATCH, POOL_HIST, POOL_WIDTH), 1.0),
        "meta_tokens": nrm(ks[5], (N_META, D_MODEL), 1.0),
        "w_in": nrm(ks[6], (DEPTH, D_MODEL, IN_WIDTH), D_MODEL ** -0.5),
        "w_out": nrm(ks[7], (DEPTH, MIX_WIDTH, D_MODEL), MIX_WIDTH ** -0.5),
        "w_pool": nrm(ks[8], (DEPTH, N_POOL_GROUPS, POOL_GROUP_DIM, POOL_GROUP_DIM), POOL_GROUP_DIM ** -0.5),
        "pool_scale": 1.0 + nrm(ks[9], (DEPTH, POOL_WIDTH), 0.1),
        "norm_gains": 1.0 + nrm(ks[10], (DEPTH, 6, D_MODEL), 0.1),
        "ffn1_gate": nrm(ks[11], (DEPTH, D_MODEL, D_FF), D_MODEL ** -0.5),
        "ffn1_up": nrm(ks[12], (DEPTH, D_MODEL, D_FF), D_MODEL ** -0.5),
        "ffn1_down": nrm(ks[13], (DEPTH, D_FF, D_MODEL), D_FF ** -0.5),
        "ffn2_gate": nrm(ks[14], (DEPTH, D_MODEL, D_FF), D_MODEL ** -0.5),
        "ffn2_up": nrm(ks[15], (DEPTH, D_MODEL, D_FF), D_MODEL ** -0.5),
        "ffn2_down": nrm(ks[16], (DEPTH, D_FF, D_MODEL), D_FF ** -0.5),
    }


def reference(x_prompt, x_sample, cache_k, cache_v, state_pool, meta_tokens,
              w_in, w_out, w_pool, pool_scale, norm_gains,
              ffn1_gate, ffn1_up, ffn1_down, ffn2_gate, ffn2_up, ffn2_down):
    B = x_prompt.shape[0]
    meta = jnp.broadcast_to(meta_tokens.astype(x_prompt.dtype)[None], (B, N_META, D_MODEL))
    xp = jnp.concatenate([meta, x_prompt], axis=1)
    xs = x_sample
    zero_hist = jnp.zeros((B, POOL_HIST, POOL_WIDTH), x_prompt.dtype)
    kp_l, vp_l, hp_l, ks_l, vs_l, hs_l = [], [], [], [], [], []
    for l in range(DEPTH):
        w = (w_in[l], w_out[l], w_pool[l], pool_scale[l], norm_gains[l],
             ffn1_gate[l], ffn1_up[l], ffn1_down[l], ffn2_gate[l], ffn2_up[l], ffn2_down[l])
        xp, kp, vp, hp = trunk_layer(xp, *w, None, None, zero_hist, 0)
        xs, k_s, v_s, h_s = trunk_layer(xs, *w, cache_k[l], cache_v[l], state_pool[l], POOL_HIST)
        kp_l.append(kp); vp_l.append(vp); hp_l.append(hp)
        ks_l.append(k_s); vs_l.append(v_s); hs_l.append(h_s)
    y_prompt = xp[:, N_META:]
    return (y_prompt, xs,
            jnp.stack(kp_l), jnp.stack(vp_l), jnp.stack(hp_l),
            jnp.stack(ks_l), jnp.stack(vs_l), jnp.stack(hs_l))
```

```python
import contextlib
import numpy as np
import concourse.bass as bass
import concourse.mybir as mybir
from concourse.bass_utils import run_bass_kernel_spmd

F32 = mybir.dt.float32
BF16 = mybir.dt.bfloat16
AF = mybir.ActivationFunctionType
ALU = mybir.AluOpType

DM = 2048
NCH = 16
DFF = 5632
NF = 44
L = 2
NT = 1296
TT = [(0, 272), (272, 384), (656, 384), (1040, 256)]
NH = 8
PAST = 4096
NEG = -30000.0
EPS = 1e-6
WINS = (2, 4, 8, 16)
N_CORES = 8

C_ID, C_NTRI, C_MASK, C_TRI, C_SEL, C_RCM, C_END = 0, 128, 256, 768, 896, 900, 964
A_H = 0
A_CST = A_H + NCH * NT
A_CB = A_CST + 1024
A_GT = A_CB + 832
A_PST = A_GT + 192
A_PH = A_PST + 16
A_END = 52900


class Buf:
    __slots__ = ("w", "r", "excl")

    def __init__(self, excl=False):
        self.w = None
        self.r = {}
        self.excl = excl


def bufs(*shape):
    if len(shape) == 1:
        return [Buf() for _ in range(shape[0])]
    return [bufs(*shape[1:]) for _ in range(shape[0])]


class Tracker:
    def __init__(self, nc, es):
        self.nc = nc
        self.E = {"pe": nc.tensor, "act": nc.scalar, "dve": nc.vector, "pool": nc.gpsimd, "sp": nc.sync}
        self.sems = {}
        self.val = {}
        for k in self.E:
            self.sems[k] = es.enter_context(nc.semaphore("c_" + k))
            self.val[k] = 0
        self.waited = {k: {} for k in self.E}
        self.slots = {}
        self.slot_i = {}
        for q, n in (("sp", 10), ("pool", 10)):
            ks = []
            for i in range(n):
                k = "d_%s_%d" % (q, i)
                self.sems[k] = es.enter_context(nc.semaphore(k))
                self.val[k] = 0
                ks.append(k)
            self.slots[q] = ks
            self.slot_i[q] = 0
        self.sems["cc"] = es.enter_context(nc.semaphore("cc"))
        self.val["cc"] = 0

    def _deps(self, reads, writes, eng=None):
        d = {}
        for b in reads:
            if b.w is not None:
                k, v = b.w
                if d.get(k, 0) < v:
                    d[k] = v
            if b.excl:
                for k, v in b.r.items():
                    if k != eng and d.get(k, 0) < v:
                        d[k] = v
        for b in writes:
            if b.w is not None:
                k, v = b.w
                if d.get(k, 0) < v:
                    d[k] = v
            for k, v in b.r.items():
                if d.get(k, 0) < v:
                    d[k] = v
        return d

    def _wait(self, eng, d):
        w = self.waited[eng]
        for k, v in d.items():
            if eng == "pe" and k == "pe":
                continue
            if w.get(k, 0) >= v:
                continue
            self.E[eng].wait_ge(self.sems[k], v)
            w[k] = v

    def _mark(self, reads, writes, t):
        k, v = t
        for b in reads:
            if b.r.get(k, 0) < v:
                b.r[k] = v
        for b in writes:
            b.w = t
            b.r = {}

    def op(self, eng, fn, reads=(), writes=()):
        self._wait(eng, self._deps(reads, writes, eng))
        inst = fn(self.E[eng])
        self.val[eng] += 1
        inst.then_inc(self.sems[eng], 1)
        self._mark(reads, writes, (eng, self.val[eng]))

    def pe_group(self, items):
        allr, allw = [], []
        inst = None
        for fn, r, w in items:
            self._wait("pe", self._deps(r, w))
            inst = fn(self.E["pe"])
            allr += list(r)
            allw += list(w)
        self.val["pe"] += 1
        inst.then_inc(self.sems["pe"], 1)
        self._mark(allr, allw, ("pe", self.val["pe"]))

    def dma(self, q, out, in_, reads=(), writes=()):
        ks = self.slots[q]
        s = ks[self.slot_i[q] % len(ks)]
        self.slot_i[q] += 1
        d = self._deps(reads, writes)
        if self.val[s] > 0:
            d[s] = max(d.get(s, 0), self.val[s])
        self._wait(q, d)
        inst = self.E[q].dma_start(out=out, in_=in_)
        self.val[s] += 16
        inst.then_inc(self.sems[s], 16)
        self._mark(reads, writes, (s, self.val[s]))

    def collective(self, ins, outs, reads=(), writes=()):
        d = self._deps(reads, writes)
        if self.val["cc"] > 0:
            d["cc"] = self.val["cc"]
        self._wait("pool", d)
        inst = self.nc.gpsimd.collective_compute(
            "AllGather", ALU.bypass, replica_groups=[[0, 1, 2, 3], [4, 5, 6, 7]], ins=ins, outs=outs)
        self.val["cc"] += 1
        inst.then_inc(self.sems["cc"], 1)
        self._mark(reads, writes, ("cc", self.val["cc"]))

    def barrier(self):
        for e in self.E:
            self._wait(e, dict((k, v) for k, v in self.val.items() if v > 0))


def build(depth=L, phases=("ffn1", "mix", "ffn2"), PAST=PAST):
    nc = bass.Bass("TRN2", target_bir_lowering=False)

    def din(name, shape, dt=F32):
        return nc.dram_tensor(name, list(shape), dt, kind="ExternalInput").ap()

    def dout(name, shape, dt=F32):
        return nc.dram_tensor(name, list(shape), dt, kind="ExternalOutput").ap()

    xT = din("xT", [NCH, 128, NT])
    ckT = din("ckT", [L, 4, NH, 128, PAST])
    cv = din("cv", [L, 4, PAST, 1024])
    spT = din("spT", [L, 4, 8, 128, 16])
    w_in = din("w_in", [L, DM, 4096])
    w_out = din("w_out", [L, DM, DM])
    w_pool = din("w_pool", [L, 4, 256, 256])
    fw = {}
    for nm, shp in (("fg1", [L, DM, DFF]), ("fu1", [L, DM, DFF]), ("fd1", [L, DFF, DM]),
                    ("fg2", [L, DM, DFF]), ("fu2", [L, DM, DFF]), ("fd2", [L, DFF, DM])):
        if ("ffn" + nm[2]) in phases:
            fw[nm] = din(nm, shp)
    gT_d = din("gT", [128, L * 96])
    psT_d = din("psT", [128, L * 8])
    cst_d = din("cst", [128, 1024])

    yT = dout("yT", [NCH, 128, NT])
    kT_out = dout("kT_out", [L, NH, 128, NT])
    v_out = dout("v_out", [L, NT, 1024])
    pool_out = dout("pool_out", [L, 5, 8, 128, 16])

    kT_c = [[nc.dram_tensor("kT_c%d_%d" % (l, i), [512, 1024], BF16).ap() for i in range(2)] for l in range(L)]
    kT_G = [[nc.dram_tensor("kT_G%d_%d" % (l, i), [2048, 1024], BF16).ap() for i in range(2)] for l in range(L)]
    v_c = [[nc.dram_tensor("v_c%d_%d" % (l, i), [512, 1024], BF16).ap() for i in range(2)] for l in range(L)]
    v_G = [[nc.dram_tensor("v_G%d_%d" % (l, i), [2048, 1024], BF16).ap() for i in range(2)] for l in range(L)]
    ut_c = [nc.dram_tensor("ut_c%d" % l, [128, 1024], F32).ap() for l in range(L)]
    ut_G = [nc.dram_tensor("ut_G%d" % l, [512, 1024], F32).ap() for l in range(L)]

    es = contextlib.ExitStack()
    with es:
        T = Tracker(nc, es)
        arena = nc.alloc_sbuf_tensor("arena", [128, A_END], F32)
        PS = [es.enter_context(nc.psum_tensor("ps%d" % i, [128, 512], F32)) for i in range(8)]
        psB = [Buf(excl=True) for _ in range(8)]

        def f32v(off, n):
            return arena[:, off:off + n]

        def bfv(off, nbf):
            return arena[:, off:off + (nbf + 1) // 2].bitcast(BF16)

        h = f32v(A_H, NCH * NT).rearrange("p (c t) -> p c t", c=NCH)
        hB = bufs(NCH, 4)
        cst = f32v(A_CST, 1024)
        cb = bfv(A_CB, 1664)
        ident_bf = cb[:, 0:128]
        negtri_bf = cb[:, 128:256]
        negones_bf = cb[:, 256:384]
        ones_bf = cb[:, 384:512]
        maskb_bf = cb[:, 512:1152].rearrange("p (v q) -> p v q", v=5)
        zeros_bf = cb[:, 1152:1664]
        gT = f32v(A_GT, 192)
        psT = f32v(A_PST, 16)
        cB = Buf()

        T.dma("sp", cst, cst_d[:, :], writes=[cB])
        T.dma("sp", gT, gT_d[:, :], writes=[cB])
        T.dma("sp", psT, psT_d[:, :], writes=[cB])
        for c in range(NCH):
            T.dma("sp", h[:, c, :], xT[c, :, :], writes=hB[c])
        T.op("dve", lambda e: e.tensor_copy(out=ident_bf, in_=cst[:, C_ID:C_ID + 128]), [cB], [cB])
        T.op("dve", lambda e: e.tensor_copy(out=negtri_bf, in_=cst[:, C_NTRI:C_NTRI + 128]), [cB], [cB])
        T.op("dve", lambda e: e.memset(negones_bf, -1.0), [], [cB])
        T.op("dve", lambda e: e.memset(ones_bf, 1.0), [], [cB])
        T.op("dve", lambda e: e.tensor_copy(out=cb[:, 512:1152], in_=cst[:, C_MASK:C_MASK + 640]), [cB], [cB])
        T.op("dve", lambda e: e.memset(zeros_bf, 0.0), [], [cB])
        T.barrier()

        def tile_cols(t):
            return TT[t][0], TT[t][1]

        def rstd_from_ss(ss_ps, ssB, n, lnv, rstd_out, half, scrB):
            T.op("act", lambda e: e.activation(out=lnv[:, :n], in_=ss_ps[:, :n], func=AF.Ln,
                                               scale=1.0 / DM, bias=EPS), [ssB], [scrB])
            b = float(np.log(0.5)) if half else 0.0
            T.op("act", lambda e: e.activation(out=rstd_out, in_=lnv[:, :n], func=AF.Exp,
                                               scale=-0.5, bias=b), [scrB], [scrB])

        def pre_norm(tiles, gcol, hn, hnB, sq, sqB, lnv, rstd, scrB, ss_bank):
            t0 = TT[tiles[0]][0]
            for ti, t in enumerate(tiles):
                s, n = tile_cols(t)
                for c in range(NCH):
                    i = c % 2
                    T.op("act", lambda e: e.activation(out=sq[i][:, :n], in_=h[:, c, s:s + n], func=AF.Square),
                         [hB[c][t]], [sqB[i]])
                    T.pe_group([(lambda e: e.matmul(PS[ss_bank][:, :n], lhsT=ones_bf, rhs=sq[i][:, :n],
                                                    start=(c == 0), stop=(c == NCH - 1)),
                                 [sqB[i], cB], [psB[ss_bank]])])
                rs = rstd[:, s - t0:s - t0 + n]
                rstd_from_ss(PS[ss_bank], psB[ss_bank], n, lnv, rs, False, scrB)
                for c in range(NCH):
                    T.op("dve", lambda e: e.scalar_tensor_tensor(
                        out=hn[:, c, s - t0:s - t0 + n], in0=h[:, c, s:s + n],
                        scalar=gT[:, gcol + c:gcol + c + 1], in1=rs, op0=ALU.mult, op1=ALU.mult),
                        [hB[c][t], scrB, cB], [hnB[c][ti]])

        def post_res(tiles, t0, y, yB, rstd, scrB, tmp, tmpB):
            for ti, t in enumerate(tiles):
                s, n = tile_cols(t)
                for c in range(NCH):
                    i = c % 2
                    T.op("dve", lambda e: e.tensor_tensor(out=tmp[i][:, :n], in0=y[:, c, s - t0:s - t0 + n],
                                                          in1=rstd[:, s - t0:s - t0 + n], op=ALU.mult),
                         [yB[c][ti], scrB], [tmpB[i]])
                    T.op("dve", lambda e: e.tensor_tensor(out=h[:, c, s:s + n], in0=h[:, c, s:s + n],
                                                          in1=tmp[i][:, :n], op=ALU.add),
                         [tmpB[i], hB[c][t]], [hB[c][t]])

        def ffn(l, which):
            Wg, Wu, Wd = (fw["fg1"], fw["fu1"], fw["fd1"]) if which == 0 else (fw["fg2"], fw["fu2"], fw["fd2"])
            gi = (l * 6 + (0 if which == 0 else 4)) * 16
            for p in range(2):
                tiles = [2 * p, 2 * p + 1]
                t0 = TT[tiles[0]][0]
                TP = TT[tiles[0]][1] + TT[tiles[1]][1]
                o = A_PH
                actT = bfv(o, NF * 656).rearrange("p (f t) -> p f t", f=NF); o += NF * 656 // 2
                actB = bufs(NF, 2)
                oG = o
                hn = bfv(o, NCH * 656).rearrange("p (c t) -> p c t", c=NCH); o += NCH * 656 // 2
                hnB = bufs(NCH, 2)
                wgu = []
                for i in range(3):
                    wgu.append(bfv(o, 2 * 16 * 128).rearrange("p (g k c) -> p g k c", g=2, k=16)); o += 2048
                wguB = bufs(3)
                tmp = [f32v(o, 384), f32v(o + 384, 384)]; o += 768
                tmpB = bufs(2)
                sq = [bfv(o, 384), bfv(o + 192, 384)]; o += 384
                sqB = bufs(2)
                lnv = f32v(o, 384); o += 384
                rstd = f32v(o, 656); o += 656
                scrB = Buf()
                assert o <= A_END

                def load_gu(f):
                    i = f % 3
                    T.dma("pool", wgu[i][:, 0, :, :],
                          Wg[l, :, f * 128:(f + 1) * 128].rearrange("(k p) c -> p k c", p=128), writes=[wguB[i]])
                    T.dma("pool", wgu[i][:, 1, :, :],
                          Wu[l, :, f * 128:(f + 1) * 128].rearrange("(k p) c -> p k c", p=128), writes=[wguB[i]])

                load_gu(0)
                load_gu(1)
                pre_norm(tiles, gi, hn, hnB, sq, sqB, lnv, rstd, scrB, 4)
                for f in range(NF):
                    if f + 2 < NF:
                        load_gu(f + 2)
                    i = f % 3
                    for ti, t in enumerate(tiles):
                        s, n = tile_cols(t)
                        ls = s - t0
                        pg = (2 * f + ti) % 2
                        pu = 2 + pg
                        T.pe_group([(lambda e, k=k: e.matmul(PS[pg][:, :n], lhsT=wgu[i][:, 0, k, :],
                                                              rhs=hn[:, k, ls:ls + n], start=(k == 0), stop=(k == 15)),
                                     [wguB[i], hnB[k][ti]], [psB[pg]]) for k in range(16)])
                        T.pe_group([(lambda e, k=k: e.matmul(PS[pu][:, :n], lhsT=wgu[i][:, 1, k, :],
                                                              rhs=hn[:, k, ls:ls + n], start=(k == 0), stop=(k == 15)),
                                     [wguB[i], hnB[k][ti]], [psB[pu]]) for k in range(16)])
                        T.op("act", lambda e: e.activation(out=tmp[pg][:, :n], in_=PS[pg][:, :n], func=AF.Silu),
                             [psB[pg]], [tmpB[pg]])
                        T.op("dve", lambda e: e.tensor_tensor(out=actT[:, f, ls:ls + n], in0=tmp[pg][:, :n],
                                                              in1=PS[pu][:, :n], op=ALU.mult),
                             [tmpB[pg], psB[pu]], [actB[f][ti]])
                T.barrier()
                o = oG
                y = f32v(o, NCH * 656).rearrange("p (c t) -> p c t", c=NCH); o += NCH * 656
                yB = bufs(NCH, 2)
                wd = []
                for i in range(2):
                    wd.append(bfv(o, 22 * 128).rearrange("p (f c) -> p f c", f=22)); o += 1408
                wdB = bufs(2)
                sq = [bfv(o, 384), bfv(o + 192, 384)]; o += 384
                sqB = bufs(2)
                lnv = f32v(o, 384); o += 384
                rstd = f32v(o, 656); o += 656
                tmp = [f32v(o, 384), f32v(o + 384, 384)]; o += 768
                tmpB = bufs(2)
                scrB = Buf()
                assert o <= A_END, o

                def load_d(idx):
                    c, hf = idx // 2, idx % 2
                    T.dma("pool", wd[idx % 2],
                          Wd[l, hf * 2816:(hf + 1) * 2816, c * 128:(c + 1) * 128].rearrange("(f p) c -> p f c", p=128),
                          writes=[wdB[idx % 2]])

                load_d(0)
                load_d(1)
                for c in range(NCH):
                    for hf in range(2):
                        idx = 2 * c + hf
                        for ti, t in enumerate(tiles):
                            s, n = tile_cols(t)
                            ls = s - t0
                            bk = (c % 2) * 2 + ti
                            T.pe_group([(lambda e, f=f: e.matmul(PS[bk][:, :n], lhsT=wd[idx % 2][:, f, :],
                                                                  rhs=actT[:, hf * 22 + f, ls:ls + n],
                                                                  start=(hf == 0 and f == 0), stop=(hf == 1 and f == 21)),
                                         [wdB[idx % 2], actB[hf * 22 + f][ti]], [psB[bk]]) for f in range(22)])
                        if idx + 2 < 2 * NCH:
                            load_d(idx + 2)
                    for ti, t in enumerate(tiles):
                        s, n = tile_cols(t)
                        ls = s - t0
                        bk = (c % 2) * 2 + ti
                        i = (2 * c + ti) % 2
                        T.op("act", lambda e: e.activation(out=y[:, c, ls:ls + n], in_=PS[bk][:, :n], func=AF.Copy,
                                                           scale=gT[:, gi + 16 + c:gi + 17 + c]),
                             [psB[bk], cB], [yB[c][ti]])
                        T.op("act", lambda e: e.activation(out=sq[i][:, :n], in_=PS[bk][:, :n], func=AF.Square),
                             [psB[bk]], [sqB[i]])
                        T.pe_group([(lambda e: e.matmul(PS[4 + ti][:, :n], lhsT=ones_bf, rhs=sq[i][:, :n],
                                                        start=(c == 0), stop=(c == NCH - 1)),
                                     [sqB[i], cB], [psB[4 + ti]])])
                for ti, t in enumerate(tiles):
                    s, n = tile_cols(t)
                    rstd_from_ss(PS[4 + ti], psB[4 + ti], n, lnv, rstd[:, s - t0:s - t0 + n], True, scrB)
                post_res(tiles, t0, y, yB, rstd, scrB, tmp, tmpB)
                T.barrier()

        def mixer(l):
            gi = (l * 6 + 2) * 16
            o = A_PH
            o_pool = bfv(o, 8 * NT).rearrange("p (c t) -> p c t", c=8); o += 8 * NT // 2
            opB = bufs(8, 4)
            kT_ms = bfv(o, 8 * 272).rearrange("p (h t) -> p h t", h=8); o += 8 * 272 // 2
            kmsB = Buf()
            v_ms = bfv(o, 5 * 1024).rearrange("p (r d) -> p r d", r=5); o += 5 * 1024 // 2
            vmsB = Buf()
            oUH = o
            uhead = f32v(o, 1024).rearrange("p (b c t) -> p b c t", b=8, c=8); o += 1024
            utail = f32v(o, 1024).rearrange("p (b c t) -> p b c t", b=8, c=8); o += 1024
            umeta = f32v(o, 128).rearrange("p (c t) -> p c t", c=8); o += 128
            uhtB = Buf()
            wp = bfv(o, 4 * 2 * 256).rearrange("p (g c d) -> p g c d", g=4, c=2); o += 1024
            wpB = Buf()
            oQ = o
            qT = bfv(o, 8 * NT).rearrange("p (h t) -> p h t", h=8); o += 8 * NT // 2
            qB = bufs(8, 4)
            oS = o
            o_sb = bfv(o, 8 * NT).rearrange("p (c t) -> p c t", c=8)
            osB = bufs(8, 4)

            T.dma("pool", wp, w_pool[l].rearrange("g (c p) d -> p g c d", p=128), writes=[wpB])

            kcB, vcB, utcB = Buf(), Buf(), Buf()
            for p in range(2):
                tiles = [2 * p, 2 * p + 1]
                t0 = TT[tiles[0]][0]
                o = oQ
                hn = bfv(o, NCH * 656).rearrange("p (c t) -> p c t", c=NCH); o += NCH * 656 // 2
                hnB = bufs(NCH, 2)
                win = []
                for i in range(2):
                    win.append(bfv(o, 16 * 128).rearrange("p (k c) -> p k c", k=16)); o += 1024
                winB = bufs(2)
                evf = [f32v(o, 384), f32v(o + 384, 384)]; o += 768
                evfB = bufs(2)
                evb = [bfv(o, 384), bfv(o + 192, 384)]; o += 384
                evbB = bufs(2)
                vtb = [bfv(o, 128), bfv(o + 64, 128)]; o += 128
                vtbB = bufs(2)
                sq = [bfv(o, 384), bfv(o + 192, 384)]; o += 384
                sqB = bufs(2)
                lnv = f32v(o, 384); o += 384
                rstd = f32v(o, 656); o += 656
                scrB = Buf()
                EW = 2 * 3 * 144
                Eb = f32v(o, EW).rearrange("p (c r x) -> p c r x", c=2, r=3); o += EW
                Ab = f32v(o, EW).rearrange("p (c r x) -> p c r x", c=2, r=3); o += EW
                Bb = f32v(o, EW).rearrange("p (c r x) -> p c r x", c=2, r=3); o += EW
                Es = f32v(o, 2 * 4 * 80).rearrange("p (c r x) -> p c r x", c=2, r=4); o += 640
                As = f32v(o, 2 * 4 * 80).rearrange("p (c r x) -> p c r x", c=2, r=4); o += 640
                Bs = f32v(o, 2 * 4 * 80).rearrange("p (c r x) -> p c r x", c=2, r=4); o += 640
                Em = f32v(o, 64).rearrange("p (c r x) -> p c r x", c=2, r=1); o += 64
                Am = f32v(o, 64).rearrange("p (c r x) -> p c r x", c=2, r=1); o += 64
                Bm = f32v(o, 64).rearrange("p (c r x) -> p c r x", c=2, r=1); o += 64
                dd = bfv(o, 2 * 384).rearrange("p (c t) -> p c t", c=2); o += 384
                ddm = bfv(o, 2 * 16).rearrange("p (c t) -> p c t", c=2); o += 16
                poolB = Buf()
                assert o <= A_END, o

                def load_win(idx, col0):
                    T.dma("pool", win[idx % 2],
                          w_in[l, :, col0:col0 + 128].rearrange("(k p) c -> p k c", p=128), writes=[winB[idx % 2]])

                def colof(idx):
                    if idx < 8:
                        return 1024 + idx * 128
                    if idx < 16:
                        return 3072 + (idx - 8) * 128
                    return 2048 + (idx - 16) * 128

                load_win(0, colof(0))
                load_win(1, colof(1))
                pre_norm(tiles, gi, hn, hnB, sq, sqB, lnv, rstd, scrB, 7)
                for bfr in (Eb, Ab, Bb, Es, As, Bs, Em, Am, Bm):
                    T.op("dve", lambda e, bfr=bfr: e.memset(bfr, 0.0), [], [poolB])
                pbank = 0
                for idx in range(24):
                    wb = win[idx % 2]
                    wB = winB[idx % 2]
                    if idx < 8:
                        hd = idx
                        for ti, t in enumerate(tiles):
                            s, n = tile_cols(t)
                            ls = s - t0
                            bk = pbank % 4; pbank += 1
                            T.pe_group([(lambda e, k=k: e.matmul(PS[bk][:, :n], lhsT=wb[:, k, :], rhs=hn[:, k, ls:ls + n],
                                                                  start=(k == 0), stop=(k == 15)),
                                         [wB, hnB[k][ti]], [psB[bk]]) for k in range(16)])
                            i = bk % 2
                            T.op("act", lambda e: e.activation(out=evf[i][:, :n], in_=PS[bk][:, :n], func=AF.Copy),
                                 [psB[bk]], [evfB[i]])
                            T.dma("sp", kT_out[l, hd, :, s:s + n], evf[i][:, :n], reads=[evfB[i]])
                            if t == 0:
                                T.op("dve", lambda e: e.tensor_copy(out=kT_ms[:, hd, 0:16], in_=PS[bk][:, 0:16]),
                                     [psB[bk]], [kmsB])
                                T.op("dve", lambda e: e.tensor_copy(out=evb[i][:, :256], in_=PS[bk][:, 16:272]),
                                     [psB[bk]], [evbB[i]])
                                T.dma("sp", kT_c[l][hd // 4][(hd % 4) * 128:(hd % 4 + 1) * 128, 0:256], evb[i][:, :256],
                                      reads=[evbB[i], kcB])
                            elif t < 3:
                                T.op("dve", lambda e: e.tensor_copy(out=evb[i][:, :n], in_=PS[bk][:, :n]),
                                     [psB[bk]], [evbB[i]])
                                c0 = s - 16
                                T.dma("sp", kT_c[l][hd // 4][(hd % 4) * 128:(hd % 4 + 1) * 128, c0:c0 + n], evb[i][:, :n],
                                      reads=[evbB[i], kcB])
                            else:
                                T.op("dve", lambda e: e.tensor_copy(out=kT_ms[:, hd, 16:272], in_=PS[bk][:, :n]),
                                     [psB[bk]], [kmsB])
                    elif idx < 16:
                        ch = idx - 8
                        g = ch // 2
                        cc = ch % 2
                        for ti, t in enumerate(tiles):
                            s, n = tile_cols(t)
                            ls = s - t0
                            bk = 4 + ti if cc == 0 else 6 + ti
                            T.pe_group([(lambda e, k=k: e.matmul(PS[bk][:, :n], lhsT=wb[:, k, :], rhs=hn[:, k, ls:ls + n],
                                                                  start=(k == 0), stop=(k == 15)),
                                         [wB, hnB[k][ti]], [psB[bk]]) for k in range(16)])
                        if cc == 1:
                            for ti, t in enumerate(tiles):
                                pool_tile(l, g, t, [PS[4 + ti], PS[6 + ti]], [psB[4 + ti], psB[6 + ti]],
                                          (Eb, Ab, Bb, Es, As, Bs, Em, Am, Bm, dd, ddm), poolB,
                                          uhead, utail, umeta, uhtB, wp, wpB, o_pool, opB)
                    else:
                        hd = idx - 16
                        for ti, t in enumerate(tiles):
                            s, n = tile_cols(t)
                            ls = s - t0
                            if t == 0:
                                blocks = [(0, 16), (16, 128), (144, 128)]
                            elif t < 3:
                                blocks = [(0, 128), (128, 128), (256, 128)]
                            else:
                                blocks = [(0, 64), (64, 64), (128, 64), (192, 64)]
                            for bi, (b0, bn) in enumerate(blocks):
                                bk = pbank % 4; pbank += 1
                                i = bk % 2
                                T.pe_group([(lambda e, k=k: e.matmul(PS[bk][:bn, :128], lhsT=hn[:, k, ls + b0:ls + b0 + bn],
                                                                      rhs=wb[:, k, :], start=(k == 0), stop=(k == 15)),
                                             [wB, hnB[k][ti]], [psB[bk]]) for k in range(16)])
                                T.op("act", lambda e: e.activation(out=evf[i][:bn, :128], in_=PS[bk][:bn, :128], func=AF.Copy),
                                     [psB[bk]], [evfB[i]])
                                T.dma("sp", v_out[l, s + b0:s + b0 + bn, hd * 128:(hd + 1) * 128], evf[i][:bn, :128],
                                      reads=[evfB[i]])
                                if t == 0 and bi == 0:
                                    T.op("dve", lambda e: e.tensor_copy(out=v_ms[:16, 0, hd * 128:(hd + 1) * 128],
                                                                        in_=PS[bk][:16, :128]), [psB[bk]], [vmsB])
                                elif t == 3:
                                    T.op("dve", lambda e: e.tensor_copy(out=v_ms[:64, 1 + bi, hd * 128:(hd + 1) * 128],
                                                                        in_=PS[bk][:64, :128]), [psB[bk]], [vmsB])
                                else:
                                    T.op("dve", lambda e: e.tensor_copy(out=vtb[i][:, :], in_=PS[bk][:, :128]),
                                         [psB[bk]], [vtbB[i]])
                                    r0 = s + b0 - 16
                                    T.dma("sp", v_c[l][r0 // 512][r0 % 512:r0 % 512 + 128, hd * 128:(hd + 1) * 128], vtb[i][:, :],
                                          reads=[vtbB[i], vcB])
                    if idx + 2 < 24:
                        load_win(idx + 2, colof(idx + 2))
                T.barrier()
            T.dma("sp", ut_c[l].rearrange("p (b c t) -> p b c t", b=8, c=8), utail, reads=[uhtB, utcB])
            kgB, vgB, utgB = Buf(), Buf(), Buf()
            T.collective([ut_c[l]], [ut_G[l]], writes=[utcB, utgB])
            for i2 in range(2):
                T.collective([kT_c[l][i2]], [kT_G[l][i2]], writes=[kcB, kgB])
            for i2 in range(2):
                T.collective([v_c[l][i2]], [v_G[l][i2]], writes=[vcB, vgB])
            T.barrier()
            for p in range(2):
                tiles = [2 * p, 2 * p + 1]
                t0 = TT[tiles[0]][0]
                o = oS
                hn = bfv(o, NCH * 656).rearrange("p (c t) -> p c t", c=NCH); o += NCH * 656 // 2
                hnB = bufs(NCH, 2)
                win = []
                for i in range(2):
                    win.append(bfv(o, 16 * 128).rearrange("p (k c) -> p k c", k=16)); o += 1024
                winB = bufs(2)
                sq = [bfv(o, 384), bfv(o + 192, 384)]; o += 384
                sqB = bufs(2)
                lnv = f32v(o, 384); o += 384
                rstd = f32v(o, 656); o += 656
                scrB = Buf()
                assert o <= A_END, o

                def load_q(idx):
                    T.dma("pool", win[idx % 2],
                          w_in[l, :, idx * 128:(idx + 1) * 128].rearrange("(k p) c -> p k c", p=128), writes=[winB[idx % 2]])

                load_q(0)
                load_q(1)
                pre_norm(tiles, gi, hn, hnB, sq, sqB, lnv, rstd, scrB, 7)
                pbank = 0
                for hd in range(8):
                    for ti, t in enumerate(tiles):
                        s, n = tile_cols(t)
                        ls = s - t0
                        bk = pbank % 4; pbank += 1
                        T.pe_group([(lambda e, k=k: e.matmul(PS[bk][:, :n], lhsT=win[hd % 2][:, k, :], rhs=hn[:, k, ls:ls + n],
                                                              start=(k == 0), stop=(k == 15)),
                                     [winB[hd % 2], hnB[k][ti]], [psB[bk]]) for k in range(16)])
                        T.op("act", lambda e: e.activation(out=qT[:, hd, s:s + n], in_=PS[bk][:, :n], func=AF.Copy,
                                                           scale=float(128 ** -0.5)), [psB[bk]], [qB[hd][t]])
                    if hd + 2 < 8:
                        load_q(hd + 2)
                T.barrier()

            halo_fix(l, oS, uhead, umeta, uhtB, utgB, wp, wpB, o_pool, opB)
            T.barrier()
            sample_att(l, oS + 8 * NT // 2, qT, qB, kT_ms, kmsB, v_ms, vmsB, o_sb, osB, oUH)
            T.barrier()
            prompt_att(l, oS + 8 * NT // 2, qT, qB, kT_ms, kmsB, v_ms, vmsB, kgB, vgB, o_sb, osB, oUH)
            T.barrier()
            go = (l * 6 + 3) * 16
            for p in range(2):
                tiles = [2 * p, 2 * p + 1]
                t0 = TT[tiles[0]][0]
                o = A_PH + 8 * NT // 2
                y = f32v(o, NCH * 656).rearrange("p (c t) -> p c t", c=NCH); o += NCH * 656
                yB = bufs(NCH, 2)
                assert o <= oS
                o = oS + 8 * NT // 2
                wo = []
                for i in range(2):
                    wo.append(bfv(o, 16 * 128).rearrange("p (k c) -> p k c", k=16)); o += 1024
                woB = bufs(2)
                sq = [bfv(o, 384), bfv(o + 192, 384)]; o += 384
                sqB = bufs(2)
                lnv = f32v(o, 384); o += 384
                rstd = f32v(o, 656); o += 656
                tmp = [f32v(o, 384), f32v(o + 384, 384)]; o += 768
                tmpB = bufs(2)
                scrB = Buf()
                assert o <= A_END, o

                def load_o(c):
                    T.dma("pool", wo[c % 2],
                          w_out[l, :, c * 128:(c + 1) * 128].rearrange("(k p) c -> p k c", p=128), writes=[woB[c % 2]])

                load_o(0)
                load_o(1)
                for c in range(NCH):
                    for ti, t in enumerate(tiles):
                        s, n = tile_cols(t)
                        ls = s - t0
                        bk = (c % 2) * 2 + ti

                        def rhs_of(k):
                            return (o_sb[:, k, s:s + n], osB[k][t]) if k < 8 else (o_pool[:, k - 8, s:s + n], opB[k - 8][t])

                        T.pe_group([(lambda e, k=k: e.matmul(PS[bk][:, :n], lhsT=wo[c % 2][:, k, :], rhs=rhs_of(k)[0],
                                                              start=(k == 0), stop=(k == 15)),
                                     [woB[c % 2], rhs_of(k)[1]], [psB[bk]]) for k in range(16)])
                        i = (2 * c + ti) % 2
                        T.op("act", lambda e: e.activation(out=y[:, c, ls:ls + n], in_=PS[bk][:, :n], func=AF.Copy,
                                                           scale=gT[:, go + c:go + c + 1]), [psB[bk], cB], [yB[c][ti]])
                        T.op("act", lambda e: e.activation(out=sq[i][:, :n], in_=PS[bk][:, :n], func=AF.Square),
                             [psB[bk]], [sqB[i]])
                        T.pe_group([(lambda e: e.matmul(PS[4 + ti][:, :n], lhsT=ones_bf, rhs=sq[i][:, :n],
                                                        start=(c == 0), stop=(c == NCH - 1)),
                                     [sqB[i], cB], [psB[4 + ti]])])
                    if c + 2 < NCH:
                        load_o(c + 2)
                for ti, t in enumerate(tiles):
                    s, n = tile_cols(t)
                    rstd_from_ss(PS[4 + ti], psB[4 + ti], n, lnv, rstd[:, s - t0:s - t0 + n], False, scrB)
                post_res(tiles, t0, y, yB, rstd, scrB, tmp, tmpB)
                T.barrier()

        def pool_tile(l, g, t, pss, pssB, bfs, poolB, uhead, utail, umeta, uhtB, wp, wpB, o_pool, opB):
            Eb, Ab, Bb, Es, As, Bs, Em, Am, Bm, dd, ddm = bfs
            w = WINS[g]
            s, n = tile_cols(t)
            runs = []
            if t == 0:
                runs.append((Em, Am, Bm, 1, 16, 0, True))
                runs.append((Eb, Ab, Bb, 2, 128, 16, False))
            elif t < 3:
                runs.append((Eb, Ab, Bb, 3, 128, 0, False))
            else:
                runs.append((Es, As, Bs, 4, 64, 0, False))
            for (E, A, B, nr, ln, c0, is_meta) in runs:
                X = 16 + ln
                for cc in range(2):
                    src = pss[cc][:, c0:c0 + nr * ln].rearrange("p (r x) -> p r x", r=nr)
                    T.op("act", lambda e: e.activation(out=E[:, cc, :nr, 16:X], in_=src, func=AF.Copy),
                         [pssB[cc]], [poolB])
                    ch = 2 * g + cc
                    if t == 3:
                        T.dma("sp", E[:, cc, :4, 0:16], spT[l, :, ch, :, :].rearrange("r p x -> p r x"), writes=[poolB])
                        T.dma("sp", pool_out[l, 0:4, ch, :, :].rearrange("r p x -> p r x"), E[:, cc, :4, 64:80],
                              reads=[poolB])
                    elif is_meta:
                        T.op("dve", lambda e: e.tensor_copy(out=umeta[:, ch, :], in_=E[:, cc, 0, 16:32]),
                             [poolB], [uhtB])
                    else:
                        jb = (s + c0 - 16) // 128
                        T.op("dve", lambda e: e.tensor_copy(out=uhead[:, jb:jb + nr, ch, :], in_=E[:, cc, :nr, 16:32]),
                             [poolB], [uhtB])
                        T.op("dve", lambda e: e.tensor_copy(out=utail[:, jb:jb + nr, ch, :], in_=E[:, cc, :nr, X - 16:X]),
                             [poolB], [uhtB])
                        if t == 2:
                            T.dma("sp", pool_out[l, 4, ch, :, :], E[:, cc, 2, X - 16:X], reads=[poolB])
                src, dst, sh = E, A, 1
                for step in range(g + 1):
                    lo = 2 * sh - 1
                    for cc in range(2):
                        T.op("dve", lambda e, src=src, dst=dst, sh=sh, lo=lo: e.tensor_tensor(
                            out=dst[:, cc, :nr, lo:X], in0=src[:, cc, :nr, lo:X], in1=src[:, cc, :nr, lo - sh:X - sh],
                            op=ALU.add), [poolB], [poolB])
                    src, dst = dst, (B if dst is A else A)
                    sh *= 2
                sw = src
                for cc in range(2):
                    if is_meta:
                        rc = cst[:, C_RCM + 16 * g:C_RCM + 16 * g + 16]
                        T.op("dve", lambda e: e.tensor_tensor(out=A[:, cc, 0, 0:16], in0=sw[:, cc, 0, 16:32], in1=rc,
                                                              op=ALU.mult), [poolB, cB], [poolB])
                        T.op("dve", lambda e: e.tensor_tensor(out=ddm[:, cc, :], in0=A[:, cc, 0, 0:16],
                                                              in1=E[:, cc, 0, 16:32], op=ALU.subtract), [poolB], [poolB])
                    else:
                        dv = dd[:, cc, 0:nr * ln].rearrange("p (r x) -> p r x", r=nr)
                        T.op("dve", lambda e: e.scalar_tensor_tensor(
                            out=dv, in0=sw[:, cc, :nr, 16:X], scalar=1.0 / w, in1=E[:, cc, :nr, 16:X],
                            op0=ALU.mult, op1=ALU.subtract), [poolB], [poolB])
                ncol = nr * ln
                for dc in range(2):
                    bk = 2 + dc
                    if is_meta:
                        rhs = [ddm[:, 0, :], ddm[:, 1, :]]
                    else:
                        rhs = [dd[:, 0, 0:ncol], dd[:, 1, 0:ncol]]
                    T.pe_group([(lambda e, cc=cc: e.matmul(PS[bk][:, :ncol], lhsT=wp[:, g, cc, dc * 128:(dc + 1) * 128],
                                                            rhs=rhs[cc], start=(cc == 0), stop=(cc == 1)),
                                 [wpB, poolB], [psB[bk]]) for cc in range(2)])
                    ch = 2 * g + dc
                    T.op("act", lambda e: e.activation(out=o_pool[:, ch, s + c0:s + c0 + ncol], in_=PS[bk][:, :ncol],
                                                       func=AF.Copy, scale=psT[:, l * 8 + ch:l * 8 + ch + 1]),
                         [psB[bk], cB], [opB[ch][t]])

        def halo_fix(l, o, uhead, umeta, uhtB, utgB, wp, wpB, o_pool, opB):
            cand = f32v(o, 4 * 1024).rearrange("p (v b c t) -> p v b c t", v=4, b=8, c=8); o += 4096
            candB = Buf()
            EW = 8 * 8 * 32
            Sx = []
            for i in range(3):
                Sx.append(f32v(o, EW).rearrange("p (b c x) -> p b c x", b=8, c=8)); o += EW
            dh = bfv(o, 8 * 8 * 16).rearrange("p (c b x) -> p c b x", c=8, b=8); o += 512
            fxB = Buf()
            assert o <= A_END, o
            G = ut_G[l]
            T.dma("sp", cand[:, 0, 1:8, :, :], G[384:512, 0:896].rearrange("p (b c t) -> p b c t", b=7, c=8),
                  reads=[utgB], writes=[candB])
            T.op("dve", lambda e: e.tensor_copy(out=cand[:, 0, 0, :, :], in_=umeta), [uhtB], [candB])
            for v in range(1, 4):
                T.dma("sp", cand[:, v, :, :, :], G[(v - 1) * 128:v * 128, :].rearrange("p (b c t) -> p b c t", b=8, c=8),
                      reads=[utgB], writes=[candB])
            E, A, B = Sx
            for i in range(3):
                T.op("dve", lambda e, i=i: e.memset(Sx[i], 0.0), [], [fxB])
            for b in range(8):
                T.op("dve", lambda e: e.tensor_scalar(out=E[:, b, :, 0:16], in0=cand[:, 0, b, :, :],
                                                      scalar1=cst[:, C_SEL:C_SEL + 1], scalar2=None, op0=ALU.mult),
                     [candB, cB], [fxB])
                for v in range(1, 4):
                    T.op("dve", lambda e, v=v: e.scalar_tensor_tensor(
                        out=E[:, b, :, 0:16], in0=cand[:, v, b, :, :], scalar=cst[:, C_SEL + v:C_SEL + v + 1],
                        in1=E[:, b, :, 0:16], op0=ALU.mult, op1=ALU.add), [candB, cB, fxB], [fxB])
                T.op("dve", lambda e: e.tensor_copy(out=E[:, b, :, 16:32], in_=uhead[:, b, :, :]), [uhtB], [fxB])
            src_, dst_, sh = E, A, 1
            for g in range(4):
                lo = 2 * sh - 1
                for b in range(8):
                    T.op("dve", lambda e, b=b: e.tensor_tensor(
                        out=dst_[:, b, :, lo:32], in0=src_[:, b, :, lo:32], in1=src_[:, b, :, lo - sh:32 - sh], op=ALU.add),
                        [fxB], [fxB])
                sw = dst_
                for cc in range(2):
                    ch = 2 * g + cc
                    T.op("dve", lambda e: e.scalar_tensor_tensor(
                        out=dh[:, ch, :, :], in0=sw[:, :, ch, 16:32], scalar=1.0 / WINS[g], in1=E[:, :, ch, 16:32],
                        op0=ALU.mult, op1=ALU.subtract), [fxB], [fxB])
                src_, dst_ = dst_, (B if dst_ is A else A)
                sh *= 2
            for g in range(4):
                for dc in range(2):
                    bk = 2 + dc
                    T.pe_group([(lambda e, cc=cc: e.matmul(PS[bk][:, :128], lhsT=wp[:, g, cc, dc * 128:(dc + 1) * 128],
                                                            rhs=dh[:, 2 * g + cc, :, :].rearrange("p b x -> p (b x)"),
                                                            start=(cc == 0), stop=(cc == 1)),
                                 [wpB, fxB], [psB[bk]]) for cc in range(2)])
                    ch = 2 * g + dc
                    dst = o_pool[:, ch, 16:16 + 1024].rearrange("p (b x) -> p b x", b=8)[:, :, 0:16]
                    T.op("act", lambda e: e.activation(out=dst, in_=PS[bk][:, :128].rearrange("p (b x) -> p b x", b=8),
                                                       func=AF.Copy, scale=psT[:, l * 8 + ch:l * 8 + ch + 1]),
                         [psB[bk], cB], [opB[ch][0], opB[ch][1], opB[ch][2]])


        def sample_att(l, o, qT, qB, kT_ms, kmsB, v_ms, vmsB, o_sb, osB, xbuf):
            kc = []
            for i in range(2):
                kc.append(bfv(o, 8 * 512).rearrange("p (h k) -> p h k", h=8)); o += 2048
            kcB_ = bufs(2)
            vc = []
            for i in range(2):
                vc.append(bfv(o, 2 * 1024).rearrange("p (t d) -> p t d", t=2)); o += 1024
            vcB_ = bufs(2)
            e_sb = f32v(o, 512); o += 512
            sp2 = [bfv(o, 512), bfv(xbuf, 512)]; o += 256
            a2 = [bfv(o, 512), bfv(xbuf + 256, 512)]; o += 256
            S = bfv(o, 512); o += 256
            eB, SB = Buf(), Buf()
            spB, aB = bufs(2), bufs(2)
            assert o <= A_END, o

            def load_k(s_, grp):
                i = (s_ * 8 + grp) % 2
                T.dma("pool", kc[i], ckT[l, s_, :, :, grp * 512:(grp + 1) * 512].rearrange("h d k -> d h k"),
                      writes=[kcB_[i]])

            def load_v(s_, pr):
                i = (s_ * 16 + pr) % 2
                T.dma("pool", vc[i], cv[l, s_, pr * 256:(pr + 1) * 256, :].rearrange("(t p) d -> p t d", p=128),
                      writes=[vcB_[i]])

            OB = 4
            NKT = PAST // 128
            TS = [-1] + list(range(NKT - 1, -1, -1))
            NTI = len(TS)
            for s_ in range(4):
                q0 = 1040 + 64 * s_
                tq = 3
                T.op("dve", lambda e: e.memset(S, 0.0), [], [SB])
                T.pe_group([(lambda e: e.matmul(PS[OB][:, :], lhsT=zeros_bf[:, 0:128], rhs=zeros_bf, start=True, stop=False),
                             [cB], [psB[OB]])])
                load_k(s_, PAST // 512 - 1)
                load_v(s_, PAST // 256 - 1)

                def stage_Z(i):
                    t = TS[i]
                    pb = i % 2
                    P = PS[pb]
                    if t >= 0:
                        grp = t // 4
                        if t % 4 == 3 and grp > 0:
                            load_k(s_, grp - 1)
                        ki = (s_ * 8 + grp) % 2
                        kr = 128
                        items = [(lambda e, hd=hd: e.matmul(P[:, hd * 64:(hd + 1) * 64],
                                                            lhsT=kc[ki][:, hd, (t % 4) * 128:(t % 4) * 128 + 128],
                                                            rhs=qT[:, hd, q0:q0 + 64], start=(hd == 0), stop=False),
                                  [kcB_[ki], qB[hd][tq]], [psB[pb]]) for hd in range(8)]
                    else:
                        kr = 64
                        items = [(lambda e, hd=hd: e.matmul(P[:64, hd * 64:(hd + 1) * 64],
                                                            lhsT=kT_ms[:, hd, 16 + 64 * s_:16 + 64 * s_ + 64],
                                                            rhs=qT[:, hd, q0:q0 + 64], start=(hd == 0), stop=False),
                                  [kmsB, qB[hd][tq]], [psB[pb]]) for hd in range(8)]
                        items += [(lambda e, hd=hd: e.matmul(P[:64, hd * 64:(hd + 1) * 64], lhsT=ident_bf[:64, :64],
                                                             rhs=maskb_bf[:64, 4, 0:64], start=False, stop=False),
                                   [cB], [psB[pb]]) for hd in range(8)]
                    T.pe_group(items)
                    T.op("act", lambda e: e.activation(out=e_sb[:kr, :], in_=P[:kr, :], func=AF.Exp), [psB[pb]], [eB])
                    T.op("act", lambda e: e.activation(out=sp2[i % 2][:kr, :], in_=e_sb[:kr, :], func=AF.Ln, bias=1.0),
                         [eB], [spB[i % 2]])

                def stage_T(i):
                    t = TS[i]
                    pb = i % 2
                    P = PS[pb]
                    kr = 128 if t >= 0 else 64
                    items = [(lambda e: e.matmul(P[:kr, :], lhsT=negtri_bf[:kr, :kr], rhs=sp2[i % 2][:kr, :], start=False,
                                                 stop=(t == -1)), [spB[i % 2], cB], [psB[pb]])]
                    if t != -1:
                        items.append((lambda e: e.matmul(P[:kr, :], lhsT=negones_bf[:, :kr], rhs=S[:, :], start=False,
                                                         stop=True), [SB, cB], [psB[pb]]))
                    T.pe_group(items)

                def stage_E2(i):
                    t = TS[i]
                    pb = i % 2
                    kr = 128 if t >= 0 else 64
                    T.op("act", lambda e: e.activation(out=a2[i % 2][:kr, :], in_=PS[pb][:kr, :], func=AF.Exp),
                         [psB[pb]], [aB[i % 2]])
                    T.op("dve", lambda e: e.tensor_tensor(out=S[:kr, :], in0=S[:kr, :], in1=sp2[i % 2][:kr, :], op=ALU.add),
                         [spB[i % 2], SB], [SB])

                def stage_AV(i):
                    t = TS[i]
                    a_sb = a2[i % 2]
                    if t >= 0:
                        pr = t // 2
                        if t % 2 == 1 and pr > 0:
                            load_v(s_, pr - 1)
                        vi = (s_ * 16 + pr) % 2
                        items = [(lambda e, hd=hd: e.matmul(PS[OB][:, hd * 64:(hd + 1) * 64],
                                                            lhsT=vc[vi][:, t % 2, hd * 128:(hd + 1) * 128],
                                                            rhs=a_sb[:, hd * 64:(hd + 1) * 64], start=False, stop=(t == 0)),
                                  [vcB_[vi], aB[i % 2]], [psB[OB]]) for hd in range(8)]
                    else:
                        items = [(lambda e, hd=hd: e.matmul(PS[OB][:, hd * 64:(hd + 1) * 64],
                                                            lhsT=v_ms[:64, 1 + s_, hd * 128:(hd + 1) * 128],
                                                            rhs=a_sb[:64, hd * 64:(hd + 1) * 64], start=False, stop=False),
                                  [vmsB, aB[i % 2]], [psB[OB]]) for hd in range(8)]
                    T.pe_group(items)

                for i in range(-1, NTI):
                    if i + 1 < NTI:
                        stage_Z(i + 1)
                    if i >= 0:
                        stage_T(i)
                    if i >= 1:
                        stage_AV(i - 1)
                    if i >= 0:
                        stage_E2(i)
                stage_AV(NTI - 1)
                T.op("act", lambda e: e.activation(out=o_sb[:, :, q0:q0 + 64],
                                                   in_=PS[OB][:, :].rearrange("p (h q) -> p h q", h=8), func=AF.Copy),
                     [psB[OB]], [osB[hd][3] for hd in range(8)])

        def prompt_att(l, o, qT, qB, kT_ms, kmsB, v_ms, vmsB, kgB, vgB, o_sb, osB, xbuf):
            kTh = bfv(o, 16 + 4096); o += (16 + 4096) // 2
            kThB = Buf()
            Vh = bfv(o, 33 * 128).rearrange("p (t d) -> p t d", t=33); o += 33 * 64
            VhB = Buf()
            e_sb = f32v(o, 1024); o += 1024
            sp2 = [bfv(o, 1024), bfv(xbuf, 1024)]; o += 512
            a2 = [bfv(o, 1024), bfv(xbuf + 512, 1024)]; o += 512
            S = bfv(o, 1024); o += 512
            em = f32v(o, 16); o += 16
            spm = bfv(o, 16); o += 8
            am = bfv(o, 16); o += 8
            eB, SB, mB = Buf(), Buf(), Buf()
            spB, aB = bufs(2), bufs(2)
            assert o <= A_END, o
            KG, VG = kT_G[l], v_G[l]
            OBK = [4, 5]
            OM = 6
            TS = list(range(31, -1, -1)) + [-1]
            NTI = len(TS)

            def segs_of(t):
                if t < 0:
                    return [(0, 0, 512), (1, 0, 512)]
                jmin = t // 4
                sg = []
                for half in range(2):
                    lo = max(jmin * 128, half * 512)
                    hi = (half + 1) * 512
                    if lo < hi:
                        sg.append((half, lo - half * 512, hi - half * 512))
                return sg

            for hd in range(8):
                for r in range(4):
                    T.dma("sp", kTh[:, 16 + r * 1024:16 + (r + 1) * 1024],
                          KG[hd // 4][r * 512 + (hd % 4) * 128:r * 512 + (hd % 4 + 1) * 128, :], reads=[kgB], writes=[kThB])
                    for i2 in range(2):
                        T.dma("sp", Vh[:, 1 + r * 8 + 4 * i2:5 + r * 8 + 4 * i2, :],
                              VG[i2][r * 512:(r + 1) * 512, hd * 128:(hd + 1) * 128].rearrange("(j p) d -> p j d", p=128),
                              reads=[vgB], writes=[VhB])
                T.op("dve", lambda e: e.tensor_copy(out=kTh[:, 0:16], in_=kT_ms[:, hd, 0:16]), [kmsB], [kThB])
                T.op("dve", lambda e: e.tensor_copy(out=Vh[:16, 0, :], in_=v_ms[:16, 0, hd * 128:(hd + 1) * 128]),
                     [vmsB], [VhB])
                T.op("dve", lambda e: e.memset(S, 0.0), [], [SB])
                for bk in OBK:
                    T.pe_group([(lambda e, bk=bk: e.matmul(PS[bk][:, :], lhsT=zeros_bf[:, 0:128], rhs=zeros_bf,
                                                            start=True, stop=False), [cB], [psB[bk]])])
                qrd = [qB[hd][0], qB[hd][1], qB[hd][2]]

                def stage_Z(i):
                    t = TS[i]
                    pset = (i % 2) * 2
                    kr = 128 if t >= 0 else 16
                    kcol = 16 + (t % 4) * 1024 + (t // 4) * 128 if t >= 0 else 0
                    items = []
                    for (half, lo, hi) in segs_of(t):
                        bk = pset + half
                        items.append((lambda e, bk=bk, half=half, lo=lo, hi=hi: e.matmul(
                            PS[bk][:kr, lo:hi], lhsT=kTh[:, kcol:kcol + kr],
                            rhs=qT[:, hd, 16 + half * 512 + lo:16 + half * 512 + hi], start=True, stop=False),
                            [kThB] + qrd, [psB[bk]]))
                    if t >= 0:
                        jmin = t // 4
                        half = jmin // 4
                        x0 = (jmin % 4) * 128
                        items.append((lambda e: e.matmul(PS[pset + half][:, x0:x0 + 128], lhsT=ident_bf,
                                                         rhs=maskb_bf[:, t % 4, :], start=False, stop=False),
                                      [cB], [psB[pset + half]]))
                    T.pe_group(items)
                    sp_sb = sp2[i % 2]
                    for (half, lo, hi) in segs_of(t):
                        bk = pset + half
                        g0 = half * 512
                        T.op("act", lambda e: e.activation(out=e_sb[:kr, g0 + lo:g0 + hi], in_=PS[bk][:kr, lo:hi],
                                                           func=AF.Exp), [psB[bk]], [eB])
                        T.op("act", lambda e: e.activation(out=sp_sb[:kr, g0 + lo:g0 + hi], in_=e_sb[:kr, g0 + lo:g0 + hi],
                                                           func=AF.Ln, bias=1.0), [eB], [spB[i % 2]])

                def stage_T(i):
                    t = TS[i]
                    pset = (i % 2) * 2
                    kr = 128 if t >= 0 else 16
                    sp_sb = sp2[i % 2]
                    items = []
                    for (half, lo, hi) in segs_of(t):
                        bk = pset + half
                        g0 = half * 512
                        items.append((lambda e, bk=bk, lo=lo, hi=hi, g0=g0: e.matmul(
                            PS[bk][:kr, lo:hi], lhsT=negtri_bf[:kr, :kr], rhs=sp_sb[:kr, g0 + lo:g0 + hi],
                            start=False, stop=False), [spB[i % 2], cB], [psB[bk]]))
                        items.append((lambda e, bk=bk, lo=lo, hi=hi, g0=g0: e.matmul(
                            PS[bk][:kr, lo:hi], lhsT=negones_bf[:, :kr], rhs=S[:, g0 + lo:g0 + hi],
                            start=False, stop=True), [SB, cB], [psB[bk]]))
                    T.pe_group(items)

                def stage_E2(i):
                    t = TS[i]
                    pset = (i % 2) * 2
                    kr = 128 if t >= 0 else 16
                    a_sb = a2[i % 2]
                    for (half, lo, hi) in segs_of(t):
                        bk = pset + half
                        g0 = half * 512
                        T.op("act", lambda e: e.activation(out=a_sb[:kr, g0 + lo:g0 + hi], in_=PS[bk][:kr, lo:hi],
                                                           func=AF.Exp), [psB[bk]], [aB[i % 2]])
                    if t >= 0:
                        qlo = (t // 4) * 128
                        sp_sb = sp2[i % 2]
                        T.op("dve", lambda e: e.tensor_tensor(out=S[:, qlo:1024], in0=S[:, qlo:1024],
                                                              in1=sp_sb[:, qlo:1024], op=ALU.add),
                             [spB[i % 2], SB], [SB])

                def stage_AV(i):
                    t = TS[i]
                    kr = 128 if t >= 0 else 16
                    a_sb = a2[i % 2]
                    vt = 1 + (t % 4) * 8 + t // 4 if t >= 0 else 0
                    items = []
                    for (half, lo, hi) in segs_of(t):
                        g0 = half * 512
                        items.append((lambda e, half=half, lo=lo, hi=hi, g0=g0: e.matmul(
                            PS[OBK[half]][:, lo:hi], lhsT=Vh[:kr, vt, :], rhs=a_sb[:kr, g0 + lo:g0 + hi],
                            start=False, stop=(t < 0)), [VhB, aB[i % 2]], [psB[OBK[half]]]))
                    T.pe_group(items)

                for i in range(-1, NTI):
                    if i + 1 < NTI:
                        stage_Z(i + 1)
                    if i >= 0:
                        stage_T(i)
                    if i >= 1:
                        stage_AV(i - 1)
                    if i >= 0:
                        stage_E2(i)
                stage_AV(NTI - 1)
                T.pe_group([(lambda e: e.matmul(PS[OM][:16, 0:16], lhsT=kTh[:, 0:16], rhs=qT[:, hd, 0:16], start=True, stop=False),
                             [kThB, qB[hd][0]], [psB[OM]]),
                            (lambda e: e.matmul(PS[OM][:16, 0:16], lhsT=ident_bf[:16, :16], rhs=maskb_bf[:16, 4, 0:16],
                                                start=False, stop=False), [cB], [psB[OM]])])
                T.op("act", lambda e: e.activation(out=em[:16, :], in_=PS[OM][:16, 0:16], func=AF.Exp), [psB[OM]], [mB])
                T.op("act", lambda e: e.activation(out=spm[:16, :], in_=em[:16, :], func=AF.Ln, bias=1.0), [mB], [mB])
                T.pe_group([(lambda e: e.matmul(PS[OM][:16, 0:16], lhsT=negtri_bf[:16, :16], rhs=spm[:16, :], start=False, stop=True),
                             [mB, cB], [psB[OM]])])
                T.op("act", lambda e: e.activation(out=am[:16, :], in_=PS[OM][:16, 0:16], func=AF.Exp), [psB[OM]], [mB])
                T.pe_group([(lambda e: e.matmul(PS[OM][:, 16:32], lhsT=Vh[:16, 0, :], rhs=am[:16, :], start=True, stop=True),
                             [VhB, mB], [psB[OM]])])
                T.op("act", lambda e: e.activation(out=o_sb[:, hd, 0:16], in_=PS[OM][:, 16:32], func=AF.Copy),
                     [psB[OM]], [osB[hd][0]])
                T.op("act", lambda e: e.activation(out=o_sb[:, hd, 16:528], in_=PS[OBK[0]][:, :], func=AF.Copy),
                     [psB[OBK[0]]], [osB[hd][0], osB[hd][1]])
                T.op("dve", lambda e: e.tensor_copy(out=o_sb[:, hd, 528:1040], in_=PS[OBK[1]][:, :]),
                     [psB[OBK[1]]], [osB[hd][1], osB[hd][2]])

        for l in range(depth):
            if "ffn1" in phases:
                ffn(l, 0)
            if "mix" in phases:
                mixer(l)
            if "ffn2" in phases:
                ffn(l, 1)
        for c in range(NCH):
            T.dma("sp", yT[c, :, :], h[:, c, :], reads=hB[c])
        T.barrier()
    return nc


def _consts(c):
    cst = np.zeros((128, 1024), np.float32)
    cst[:, C_ID:C_ID + 128] = np.eye(128, dtype=np.float32)
    j = np.arange(128)[:, None]
    k = np.arange(128)[None, :]
    cst[:, C_NTRI:C_NTRI + 128] = np.where(j >= k, -1.0, 0.0)
    tri = np.where(j < k, 0.0, NEG).astype(np.float32)
    for v in range(4):
        if v < c:
            m = np.zeros((128, 128), np.float32)
        elif v == c:
            m = tri
        else:
            m = np.full((128, 128), NEG, np.float32)
        cst[:, C_MASK + v * 128:C_MASK + (v + 1) * 128] = m
    cst[:, C_MASK + 512:C_MASK + 640] = tri
    cst[:, C_SEL + c] = 1.0
    for g, w in enumerate(WINS):
        cst[:, C_RCM + 16 * g:C_RCM + 16 * g + 16] = 1.0 / np.minimum(float(w), np.arange(16) + 1.0)
    return cst


_NC_CACHE = {}


def kernel(x_prompt, x_sample, cache_k, cache_v, state_pool, meta_tokens,
           w_in, w_out, w_pool, pool_scale, norm_gains,
           ffn1_gate, ffn1_up, ffn1_down, ffn2_gate, ffn2_up, ffn2_down, _depth=L,
           _phases=("ffn1", "mix", "ffn2"), _past=PAST, _ncores=N_CORES):
    f = np.float32
    x_prompt = np.asarray(x_prompt, f); x_sample = np.asarray(x_sample, f)
    cache_k = np.asarray(cache_k, f); cache_v = np.asarray(cache_v, f)
    state_pool = np.asarray(state_pool, f); meta_tokens = np.asarray(meta_tokens, f)
    shared = {"w_in": np.ascontiguousarray(w_in, f), "w_out": np.ascontiguousarray(w_out, f),
              "w_pool": np.ascontiguousarray(w_pool, f)}
    if "ffn1" in _phases:
        shared.update({"fg1": np.ascontiguousarray(ffn1_gate, f), "fu1": np.ascontiguousarray(ffn1_up, f),
                       "fd1": np.ascontiguousarray(ffn1_down, f)})
    if "ffn2" in _phases:
        shared.update({"fg2": np.ascontiguousarray(ffn2_gate, f), "fu2": np.ascontiguousarray(ffn2_up, f),
                       "fd2": np.ascontiguousarray(ffn2_down, f)})
    ng = np.asarray(norm_gains, f)
    gT = np.ascontiguousarray(ng.reshape(L, 6, NCH, 128).transpose(3, 0, 1, 2).reshape(128, L * 96))
    ps = np.asarray(pool_scale, f)
    psT = np.ascontiguousarray(ps.reshape(L, 8, 128).transpose(2, 0, 1).reshape(128, L * 8))
    shared["gT"] = gT
    shared["psT"] = psT
    in_maps = []
    for r in range(N_CORES):
        b, c = r // 4, r % 4
        cols = [meta_tokens]
        for j in range(8):
            g = 4 * j + c
            cols.append(x_prompt[b, 128 * g:128 * (g + 1)])
        for s in range(4):
            cols.append(x_sample[4 * r + s])
        xl = np.concatenate(cols, axis=0)
        xT = np.ascontiguousarray(xl.T.reshape(NCH, 128, NT))
        ck = cache_k[:, 4 * r:4 * r + 4, PAST - _past:]
        ckT = np.ascontiguousarray(ck.transpose(0, 1, 3, 4, 2))
        cvv = np.ascontiguousarray(cache_v[:, 4 * r:4 * r + 4, PAST - _past:].reshape(L, 4, _past, 1024))
        sp = state_pool[:, 4 * r:4 * r + 4]
        spT = np.zeros((L, 4, 8, 128, 16), f)
        spT[..., 1:] = sp.reshape(L, 4, 15, 8, 128).transpose(0, 1, 3, 4, 2)
        m = {"xT": xT, "ckT": ckT, "cv": cvv, "spT": spT, "cst": _consts(c)}
        m.update(shared)
        in_maps.append(m)
    key = (_depth, tuple(_phases), _past)
    if key not in _NC_CACHE:
        _NC_CACHE[key] = build(_depth, tuple(_phases), _past)
    nc = _NC_CACHE[key]
    res = run_bass_kernel_spmd(nc, in_maps[:_ncores], core_ids=list(range(_ncores)))
    R = list(res.results) + [res.results[0]] * (N_CORES - _ncores)
    y_prompt = np.zeros((2, 4096, DM), f)
    y_sample = np.zeros((32, 64, DM), f)
    nkp = np.zeros((L, 2, 16 + 4096, NH, 128), f)
    nvp = np.zeros((L, 2, 16 + 4096, NH, 128), f)
    npp = np.zeros((L, 2, 15, 1024), f)
    nks = np.zeros((L, 32, 64, NH, 128), f)
    nvs = np.zeros((L, 32, 64, NH, 128), f)
    nps = np.zeros((L, 32, 15, 1024), f)
    for r in range(N_CORES):
        b, c = r // 4, r % 4
        yl = R[r]["yT"].reshape(DM, NT).T
        kl = R[r]["kT_out"].reshape(L, NH * 128, NT).transpose(0, 2, 1).reshape(L, NT, NH, 128)
        vl = R[r]["v_out"].reshape(L, NT, NH, 128)
        pl = R[r]["pool_out"].reshape(L, 5, 1024, 16).transpose(0, 1, 3, 2)
        for j in range(8):
            g = 4 * j + c
            y_prompt[b, 128 * g:128 * (g + 1)] = yl[16 + 128 * j:16 + 128 * (j + 1)]
            nkp[:, b, 16 + 128 * g:16 + 128 * (g + 1)] = kl[:, 16 + 128 * j:16 + 128 * (j + 1)]
            nvp[:, b, 16 + 128 * g:16 + 128 * (g + 1)] = vl[:, 16 + 128 * j:16 + 128 * (j + 1)]
        if c == 0:
            nkp[:, b, 0:16] = kl[:, 0:16]
            nvp[:, b, 0:16] = vl[:, 0:16]
        if c == 3:
            npp[:, b] = pl[:, 4, 1:16]
        for s in range(4):
            y_sample[4 * r + s] = yl[1040 + 64 * s:1040 + 64 * (s + 1)]
            nks[:, 4 * r + s] = kl[:, 1040 + 64 * s:1040 + 64 * (s + 1)]
            nvs[:, 4 * r + s] = vl[:, 1040 + 64 * s:1040 + 64 * (s + 1)]
            nps[:, 4 * r + s] = pl[:, s, 1:16]
    return (y_prompt, y_sample, nkp, nvp, npp, nks, nvs, nps)
```

```python
import contextlib
import numpy as np
import concourse.bass as bass
import concourse.mybir as mybir
from concourse.bass_utils import run_bass_kernel_spmd

F32 = mybir.dt.float32
BF16 = mybir.dt.bfloat16
AF = mybir.ActivationFunctionType
ALU = mybir.AluOpType

DM = 2048
NCH = 16
DFF = 5632
NF = 44
L = 2
NT = 1296
TT = [(0, 272), (272, 384), (656, 384), (1040, 256)]
NH = 8
PAST = 4096
NEG = -30000.0
EPS = 1e-6
WINS = (2, 4, 8, 16)
N_CORES = 8

C_ID, C_NTRI, C_MASK, C_TRI, C_SEL, C_RCM, C_END = 0, 128, 256, 768, 896, 900, 964
A_H = 0
A_CST = A_H + NCH * NT
A_CB = A_CST + 1024
A_GT = A_CB + 832
A_PST = A_GT + 192
A_PH = A_PST + 16
A_END = 52900


class Buf:
    __slots__ = ("w", "r", "excl")

    def __init__(self, excl=False):
        self.w = None
        self.r = {}
        self.excl = excl


def bufs(*shape):
    if len(shape) == 1:
        return [Buf() for _ in range(shape[0])]
    return [bufs(*shape[1:]) for _ in range(shape[0])]


class Tracker:
    def __init__(self, nc, es):
        self.nc = nc
        self.E = {"pe": nc.tensor, "act": nc.scalar, "dve": nc.vector, "pool": nc.gpsimd, "sp": nc.sync}
        self.sems = {}
        self.val = {}
        for k in self.E:
            self.sems[k] = es.enter_context(nc.semaphore("c_" + k))
            self.val[k] = 0
        self.waited = {k: {} for k in self.E}
        self.slots = {}
        self.slot_i = {}
        for q, n in (("sp", 16), ("pool", 16)):
            ks = []
            for i in range(n):
                k = "d_%s_%d" % (q, i)
                self.sems[k] = es.enter_context(nc.semaphore(k))
                self.val[k] = 0
                ks.append(k)
            self.slots[q] = ks
            self.slot_i[q] = 0
        self.sems["cc"] = es.enter_context(nc.semaphore("cc"))
        self.val["cc"] = 0

    def _deps(self, reads, writes, eng=None):
        d = {}
        for b in reads:
            if b.w is not None:
                k, v = b.w
                if d.get(k, 0) < v:
                    d[k] = v
            if b.excl:
                for k, v in b.r.items():
                    if k != eng and d.get(k, 0) < v:
                        d[k] = v
        for b in writes:
            if b.w is not None:
                k, v = b.w
                if d.get(k, 0) < v:
                    d[k] = v
            for k, v in b.r.items():
                if d.get(k, 0) < v:
                    d[k] = v
        return d

    def _wait(self, eng, d):
        w = self.waited[eng]
        for k, v in d.items():
            if eng == "pe" and k == "pe":
                continue
            if w.get(k, 0) >= v:
                continue
            self.E[eng].wait_ge(self.sems[k], v)
            w[k] = v

    def _mark(self, reads, writes, t):
        k, v = t
        for b in reads:
            if b.r.get(k, 0) < v:
                b.r[k] = v
        for b in writes:
            b.w = t
            b.r = {}

    def op(self, eng, fn, reads=(), writes=()):
        self._wait(eng, self._deps(reads, writes, eng))
        inst = fn(self.E[eng])
        self.val[eng] += 1
        inst.then_inc(self.sems[eng], 1)
        self._mark(reads, writes, (eng, self.val[eng]))

    def pe_group(self, items):
        allr, allw = [], []
        inst = None
        for fn, r, w in items:
            self._wait("pe", self._deps(r, w))
            inst = fn(self.E["pe"])
            allr += list(r)
            allw += list(w)
        self.val["pe"] += 1
        inst.then_inc(self.sems["pe"], 1)
        self._mark(allr, allw, ("pe", self.val["pe"]))

    def dma(self, q, out, in_, reads=(), writes=()):
        ks = self.slots[q]
        s = ks[self.slot_i[q] % len(ks)]
        self.slot_i[q] += 1
        d = self._deps(reads, writes)
        if self.val[s] > 0:
            d[s] = max(d.get(s, 0), self.val[s])
        self._wait(q, d)
        inst = self.E[q].dma_start(out=out, in_=in_)
        self.val[s] += 16
        inst.then_inc(self.sems[s], 16)
        self._mark(reads, writes, (s, self.val[s]))

    def collective(self, ins, outs, reads=(), writes=()):
        d = self._deps(reads, writes)
        if self.val["cc"] > 0:
            d["cc"] = self.val["cc"]
        self._wait("pool", d)
        inst = self.nc.gpsimd.collective_compute(
            "AllGather", ALU.bypass, replica_groups=[[0, 1, 2, 3], [4, 5, 6, 7]], ins=ins, outs=outs)
        self.val["cc"] += 1
        inst.then_inc(self.sems["cc"], 1)
        self._mark(reads, writes, ("cc", self.val["cc"]))

    def barrier(self):
        for e in self.E:
            self._wait(e, dict((k, v) for k, v in self.val.items() if v > 0))


def build(depth=L, phases=("ffn1", "mix", "ffn2"), PAST=PAST):
    nc = bass.Bass("TRN2", target_bir_lowering=False)

    def din(name, shape, dt=F32):
        return nc.dram_tensor(name, list(shape), dt, kind="ExternalInput").ap()

    def dout(name, shape, dt=F32):
        return nc.dram_tensor(name, list(shape), dt, kind="ExternalOutput").ap()

    xT = din("xT", [NCH, 128, NT])
    ckT = din("ckT", [L, 4, NH, 128, PAST])
    cv = din("cv", [L, 4, PAST, 1024])
    spT = din("spT", [L, 4, 8, 128, 16])
    w_in = din("w_in", [L, DM, 4096])
    w_out = din("w_out", [L, DM, DM])
    w_pool = din("w_pool", [L, 4, 256, 256])
    fw = {}
    for nm, shp in (("fg1", [L, DM, DFF]), ("fu1", [L, DM, DFF]), ("fd1", [L, DFF, DM]),
                    ("fg2", [L, DM, DFF]), ("fu2", [L, DM, DFF]), ("fd2", [L, DFF, DM])):
        if ("ffn" + nm[2]) in phases:
            fw[nm] = din(nm, shp)
    gT_d = din("gT", [128, L * 96])
    psT_d = din("psT", [128, L * 8])
    cst_d = din("cst", [128, 1024])

    yT = dout("yT", [NCH, 128, NT])
    kT_out = dout("kT_out", [L, NH, 128, NT])
    v_out = dout("v_out", [L, NT, 1024])
    pool_out = dout("pool_out", [L, 5, 8, 128, 16])

    kT_c = [[nc.dram_tensor("kT_c%d_%d" % (l, i), [512, 1024], BF16).ap() for i in range(2)] for l in range(L)]
    kT_G = [[nc.dram_tensor("kT_G%d_%d" % (l, i), [2048, 1024], BF16).ap() for i in range(2)] for l in range(L)]
    v_c = [[nc.dram_tensor("v_c%d_%d" % (l, i), [512, 1024], BF16).ap() for i in range(2)] for l in range(L)]
    v_G = [[nc.dram_tensor("v_G%d_%d" % (l, i), [2048, 1024], BF16).ap() for i in range(2)] for l in range(L)]
    ut_c = [nc.dram_tensor("ut_c%d" % l, [128, 1024], F32).ap() for l in range(L)]
    ut_G = [nc.dram_tensor("ut_G%d" % l, [512, 1024], F32).ap() for l in range(L)]

    es = contextlib.ExitStack()
    with es:
        T = Tracker(nc, es)
        arena = nc.alloc_sbuf_tensor("arena", [128, A_END], F32)
        PS = [es.enter_context(nc.psum_tensor("ps%d" % i, [128, 512], F32)) for i in range(8)]
        psB = [Buf(excl=True) for _ in range(8)]

        def f32v(off, n):
            return arena[:, off:off + n]

        def bfv(off, nbf):
            return arena[:, off:off + (nbf + 1) // 2].bitcast(BF16)

        h = f32v(A_H, NCH * NT).rearrange("p (c t) -> p c t", c=NCH)
        hB = bufs(NCH, 4)
        cst = f32v(A_CST, 1024)
        cb = bfv(A_CB, 1664)
        ident_bf = cb[:, 0:128]
        negtri_bf = cb[:, 128:256]
        negones_bf = cb[:, 256:384]
        ones_bf = cb[:, 384:512]
        maskb_bf = cb[:, 512:1152].rearrange("p (v q) -> p v q", v=5)
        zeros_bf = cb[:, 1152:1664]
        gT = f32v(A_GT, 192)
        psT = f32v(A_PST, 16)
        cB = Buf()

        T.dma("sp", cst, cst_d[:, :], writes=[cB])
        T.dma("sp", gT, gT_d[:, :], writes=[cB])
        T.dma("sp", psT, psT_d[:, :], writes=[cB])
        for c in range(NCH):
            T.dma("sp", h[:, c, :], xT[c, :, :], writes=hB[c])
        T.op("dve", lambda e: e.tensor_copy(out=ident_bf, in_=cst[:, C_ID:C_ID + 128]), [cB], [cB])
        T.op("dve", lambda e: e.tensor_copy(out=negtri_bf, in_=cst[:, C_NTRI:C_NTRI + 128]), [cB], [cB])
        T.op("dve", lambda e: e.memset(negones_bf, -1.0), [], [cB])
        T.op("dve", lambda e: e.memset(ones_bf, 1.0), [], [cB])
        T.op("dve", lambda e: e.tensor_copy(out=cb[:, 512:1152], in_=cst[:, C_MASK:C_MASK + 640]), [cB], [cB])
        T.op("dve", lambda e: e.memset(zeros_bf, 0.0), [], [cB])
        T.barrier()

        def tile_cols(t):
            return TT[t][0], TT[t][1]

        def rstd_from_ss(ss_ps, ssB, n, lnv, rstd_out, half, scrB):
            T.op("act", lambda e: e.activation(out=lnv[:, :n], in_=ss_ps[:, :n], func=AF.Ln,
                                               scale=1.0 / DM, bias=EPS), [ssB], [scrB])
            b = float(np.log(0.5)) if half else 0.0
            T.op("act", lambda e: e.activation(out=rstd_out, in_=lnv[:, :n], func=AF.Exp,
                                               scale=-0.5, bias=b), [scrB], [scrB])

        def pre_norm(tiles, gcol, hn, hnB, sq, sqB, lnv, rstd, scrB, ss_bank):
            t0 = TT[tiles[0]][0]
            for ti, t in enumerate(tiles):
                s, n = tile_cols(t)
                for c in range(NCH):
                    i = c % 2
                    T.op("act", lambda e: e.activation(out=sq[i][:, :n], in_=h[:, c, s:s + n], func=AF.Square),
                         [hB[c][t]], [sqB[i]])
                    T.pe_group([(lambda e: e.matmul(PS[ss_bank][:, :n], lhsT=ones_bf, rhs=sq[i][:, :n],
                                                    start=(c == 0), stop=(c == NCH - 1)),
                                 [sqB[i], cB], [psB[ss_bank]])])
                rs = rstd[:, s - t0:s - t0 + n]
                rstd_from_ss(PS[ss_bank], psB[ss_bank], n, lnv, rs, False, scrB)
                for c in range(NCH):
                    T.op("dve", lambda e: e.scalar_tensor_tensor(
                        out=hn[:, c, s - t0:s - t0 + n], in0=h[:, c, s:s + n],
                        scalar=gT[:, gcol + c:gcol + c + 1], in1=rs, op0=ALU.mult, op1=ALU.mult),
                        [hB[c][t], scrB, cB], [hnB[c][ti]])

        def post_res(tiles, t0, y, yB, rstd, scrB, tmp, tmpB):
            for ti, t in enumerate(tiles):
                s, n = tile_cols(t)
                for c in range(NCH):
                    i = c % 2
                    T.op("dve", lambda e: e.tensor_tensor(out=tmp[i][:, :n], in0=y[:, c, s - t0:s - t0 + n],
                                                          in1=rstd[:, s - t0:s - t0 + n], op=ALU.mult),
                         [yB[c][ti], scrB], [tmpB[i]])
                    T.op("dve", lambda e: e.tensor_tensor(out=h[:, c, s:s + n], in0=h[:, c, s:s + n],
                                                          in1=tmp[i][:, :n], op=ALU.add),
                         [tmpB[i], hB[c][t]], [hB[c][t]])

        def ffn(l, which):
            Wg, Wu, Wd = (fw["fg1"], fw["fu1"], fw["fd1"]) if which == 0 else (fw["fg2"], fw["fu2"], fw["fd2"])
            gi = (l * 6 + (0 if which == 0 else 4)) * 16
            for p in range(2):
                tiles = [2 * p, 2 * p + 1]
                t0 = TT[tiles[0]][0]
                TP = TT[tiles[0]][1] + TT[tiles[1]][1]
                o = A_PH
                actT = bfv(o, NF * 656).rearrange("p (f t) -> p f t", f=NF); o += NF * 656 // 2
                actB = bufs(NF, 2)
                oG = o
                hn = bfv(o, NCH * 656).rearrange("p (c t) -> p c t", c=NCH); o += NCH * 656 // 2
                hnB = bufs(NCH, 2)
                wgu = []
                for i in range(3):
                    wgu.append(bfv(o, 2 * 16 * 128).rearrange("p (g k c) -> p g k c", g=2, k=16)); o += 2048
                wguB = bufs(3)
                tmp = [f32v(o, 384), f32v(o + 384, 384)]; o += 768
                tmpB = bufs(2)
                sq = [bfv(o, 384), bfv(o + 192, 384)]; o += 384
                sqB = bufs(2)
                lnv = f32v(o, 384); o += 384
                rstd = f32v(o, 656); o += 656
                scrB = Buf()
                assert o <= A_END

                def load_gu(f):
                    i = f % 3
                    T.dma("pool", wgu[i][:, 0, :, :],
                          Wg[l, :, f * 128:(f + 1) * 128].rearrange("(k p) c -> p k c", p=128), writes=[wguB[i]])
                    T.dma("pool", wgu[i][:, 1, :, :],
                          Wu[l, :, f * 128:(f + 1) * 128].rearrange("(k p) c -> p k c", p=128), writes=[wguB[i]])

                load_gu(0)
                load_gu(1)
                pre_norm(tiles, gi, hn, hnB, sq, sqB, lnv, rstd, scrB, 4)
                for f in range(NF):
                    if f + 2 < NF:
                        load_gu(f + 2)
                    i = f % 3
                    for ti, t in enumerate(tiles):
                        s, n = tile_cols(t)
                        ls = s - t0
                        pg = (2 * f + ti) % 2
                        pu = 2 + pg
                        T.pe_group([(lambda e, k=k: e.matmul(PS[pg][:, :n], lhsT=wgu[i][:, 0, k, :],
                                                              rhs=hn[:, k, ls:ls + n], start=(k == 0), stop=(k == 15)),
                                     [wguB[i], hnB[k][ti]], [psB[pg]]) for k in range(16)])
                        T.pe_group([(lambda e, k=k: e.matmul(PS[pu][:, :n], lhsT=wgu[i][:, 1, k, :],
                                                              rhs=hn[:, k, ls:ls + n], start=(k == 0), stop=(k == 15)),
                                     [wguB[i], hnB[k][ti]], [psB[pu]]) for k in range(16)])
                        T.op("act", lambda e: e.activation(out=tmp[pg][:, :n], in_=PS[pg][:, :n], func=AF.Silu),
                             [psB[pg]], [tmpB[pg]])
                        T.op("dve", lambda e: e.tensor_tensor(out=actT[:, f, ls:ls + n], in0=tmp[pg][:, :n],
                                                              in1=PS[pu][:, :n], op=ALU.mult),
                             [tmpB[pg], psB[pu]], [actB[f][ti]])
                T.barrier()
                o = oG
                y = f32v(o, NCH * 656).rearrange("p (c t) -> p c t", c=NCH); o += NCH * 656
                yB = bufs(NCH, 2)
                wd = []
                for i in range(2):
                    wd.append(bfv(o, 22 * 128).rearrange("p (f c) -> p f c", f=22)); o += 1408
                wdB = bufs(2)
                sq = [bfv(o, 384), bfv(o + 192, 384)]; o += 384
                sqB = bufs(2)
                lnv = f32v(o, 384); o += 384
                rstd = f32v(o, 656); o += 656
                tmp = [f32v(o, 384), f32v(o + 384, 384)]; o += 768
                tmpB = bufs(2)
                scrB = Buf()
                assert o <= A_END, o

                def load_d(idx):
                    c, hf = idx // 2, idx % 2
                    T.dma("pool", wd[idx % 2],
                          Wd[l, hf * 2816:(hf + 1) * 2816, c * 128:(c + 1) * 128].rearrange("(f p) c -> p f c", p=128),
                          writes=[wdB[idx % 2]])

                load_d(0)
                load_d(1)
                for c in range(NCH):
                    for hf in range(2):
                        idx = 2 * c + hf
                        for ti, t in enumerate(tiles):
                            s, n = tile_cols(t)
                            ls = s - t0
                            bk = (c % 2) * 2 + ti
                            T.pe_group([(lambda e, f=f: e.matmul(PS[bk][:, :n], lhsT=wd[idx % 2][:, f, :],
                                                                  rhs=actT[:, hf * 22 + f, ls:ls + n],
                                                                  start=(hf == 0 and f == 0), stop=(hf == 1 and f == 21)),
                                         [wdB[idx % 2], actB[hf * 22 + f][ti]], [psB[bk]]) for f in range(22)])
                        if idx + 2 < 2 * NCH:
                            load_d(idx + 2)
                    for ti, t in enumerate(tiles):
                        s, n = tile_cols(t)
                        ls = s - t0
                        bk = (c % 2) * 2 + ti
                        i = (2 * c + ti) % 2
                        T.op("act", lambda e: e.activation(out=y[:, c, ls:ls + n], in_=PS[bk][:, :n], func=AF.Copy,
                                                           scale=gT[:, gi + 16 + c:gi + 17 + c]),
                             [psB[bk], cB], [yB[c][ti]])
                        T.op("act", lambda e: e.activation(out=sq[i][:, :n], in_=PS[bk][:, :n], func=AF.Square),
                             [psB[bk]], [sqB[i]])
                        T.pe_group([(lambda e: e.matmul(PS[4 + ti][:, :n], lhsT=ones_bf, rhs=sq[i][:, :n],
                                                        start=(c == 0), stop=(c == NCH - 1)),
                                     [sqB[i], cB], [psB[4 + ti]])])
                for ti, t in enumerate(tiles):
                    s, n = tile_cols(t)
                    rstd_from_ss(PS[4 + ti], psB[4 + ti], n, lnv, rstd[:, s - t0:s - t0 + n], True, scrB)
                post_res(tiles, t0, y, yB, rstd, scrB, tmp, tmpB)
                T.barrier()

        def mixer(l):
            gi = (l * 6 + 2) * 16
            o = A_PH
            o_pool = bfv(o, 8 * NT).rearrange("p (c t) -> p c t", c=8); o += 8 * NT // 2
            opB = bufs(8, 4)
            kT_ms = bfv(o, 8 * 272).rearrange("p (h t) -> p h t", h=8); o += 8 * 272 // 2
            kmsB = Buf()
            v_ms = bfv(o, 5 * 1024).rearrange("p (r d) -> p r d", r=5); o += 5 * 1024 // 2
            vmsB = Buf()
            oUH = o
            uhead = f32v(o, 1024).rearrange("p (b c t) -> p b c t", b=8, c=8); o += 1024
            utail = f32v(o, 1024).rearrange("p (b c t) -> p b c t", b=8, c=8); o += 1024
            umeta = f32v(o, 128).rearrange("p (c t) -> p c t", c=8); o += 128
            uhtB = Buf()
            wp = bfv(o, 4 * 2 * 256).rearrange("p (g c d) -> p g c d", g=4, c=2); o += 1024
            wpB = Buf()
            oQ = o
            qT = bfv(o, 8 * NT).rearrange("p (h t) -> p h t", h=8); o += 8 * NT // 2
            qB = bufs(8, 4)
            oS = o
            o_sb = bfv(o, 8 * NT).rearrange("p (c t) -> p c t", c=8)
            osB = bufs(8, 4)

            T.dma("pool", wp, w_pool[l].rearrange("g (c p) d -> p g c d", p=128), writes=[wpB])

            kcB, vcB, utcB = Buf(), Buf(), Buf()
            for p in range(2):
                tiles = [2 * p, 2 * p + 1]
                t0 = TT[tiles[0]][0]
                o = oQ
                hn = bfv(o, NCH * 656).rearrange("p (c t) -> p c t", c=NCH); o += NCH * 656 // 2
                hnB = bufs(NCH, 2)
                win = []
                for i in range(2):
                    win.append(bfv(o, 16 * 128).rearrange("p (k c) -> p k c", k=16)); o += 1024
                winB = bufs(2)
                evf = [f32v(o, 384), f32v(o + 384, 384)]; o += 768
                evfB = bufs(2)
                evb = [bfv(o, 384), bfv(o + 192, 384)]; o += 384
                evbB = bufs(2)
                vtb = [bfv(o, 128), bfv(o + 64, 128)]; o += 128
                vtbB = bufs(2)
                sq = [bfv(o, 384), bfv(o + 192, 384)]; o += 384
                sqB = bufs(2)
                lnv = f32v(o, 384); o += 384
                rstd = f32v(o, 656); o += 656
                scrB = Buf()
                EW = 2 * 3 * 144
                Eb = f32v(o, EW).rearrange("p (c r x) -> p c r x", c=2, r=3); o += EW
                Ab = f32v(o, EW).rearrange("p (c r x) -> p c r x", c=2, r=3); o += EW
                Bb = f32v(o, EW).rearrange("p (c r x) -> p c r x", c=2, r=3); o += EW
                Es = f32v(o, 2 * 4 * 80).rearrange("p (c r x) -> p c r x", c=2, r=4); o += 640
                As = f32v(o, 2 * 4 * 80).rearrange("p (c r x) -> p c r x", c=2, r=4); o += 640
                Bs = f32v(o, 2 * 4 * 80).rearrange("p (c r x) -> p c r x", c=2, r=4); o += 640
                Em = f32v(o, 64).rearrange("p (c r x) -> p c r x", c=2, r=1); o += 64
                Am = f32v(o, 64).rearrange("p (c r x) -> p c r x", c=2, r=1); o += 64
                Bm = f32v(o, 64).rearrange("p (c r x) -> p c r x", c=2, r=1); o += 64
                dd = bfv(o, 2 * 384).rearrange("p (c t) -> p c t", c=2); o += 384
                ddm = bfv(o, 2 * 16).rearrange("p (c t) -> p c t", c=2); o += 16
                poolB = Buf()
                assert o <= A_END, o

                def load_win(idx, col0):
                    T.dma("pool", win[idx % 2],
                          w_in[l, :, col0:col0 + 128].rearrange("(k p) c -> p k c", p=128), writes=[winB[idx % 2]])

                def colof(idx):
                    if idx < 8:
                        return 1024 + idx * 128
                    if idx < 16:
                        return 3072 + (idx - 8) * 128
                    return 2048 + (idx - 16) * 128

                load_win(0, colof(0))
                load_win(1, colof(1))
                pre_norm(tiles, gi, hn, hnB, sq, sqB, lnv, rstd, scrB, 7)
                for bfr in (Eb, Ab, Bb, Es, As, Bs, Em, Am, Bm):
                    T.op("dve", lambda e, bfr=bfr: e.memset(bfr, 0.0), [], [poolB])
                pbank = 0
                for idx in range(24):
                    wb = win[idx % 2]
                    wB = winB[idx % 2]
                    if idx < 8:
                        hd = idx
                        for ti, t in enumerate(tiles):
                            s, n = tile_cols(t)
                            ls = s - t0
                            bk = pbank % 4; pbank += 1
                            T.pe_group([(lambda e, k=k: e.matmul(PS[bk][:, :n], lhsT=wb[:, k, :], rhs=hn[:, k, ls:ls + n],
                                                                  start=(k == 0), stop=(k == 15)),
                                         [wB, hnB[k][ti]], [psB[bk]]) for k in range(16)])
                            i = bk % 2
                            T.op("act", lambda e: e.activation(out=evf[i][:, :n], in_=PS[bk][:, :n], func=AF.Copy),
                                 [psB[bk]], [evfB[i]])
                            T.dma("sp", kT_out[l, hd, :, s:s + n], evf[i][:, :n], reads=[evfB[i]])
                            if t == 0:
                                T.op("dve", lambda e: e.tensor_copy(out=kT_ms[:, hd, 0:16], in_=PS[bk][:, 0:16]),
                                     [psB[bk]], [kmsB])
                                T.op("dve", lambda e: e.tensor_copy(out=evb[i][:, :256], in_=PS[bk][:, 16:272]),
                                     [psB[bk]], [evbB[i]])
                                T.dma("sp", kT_c[l][hd // 4][(hd % 4) * 128:(hd % 4 + 1) * 128, 0:256], evb[i][:, :256],
                                      reads=[evbB[i], kcB])
                            elif t < 3:
                                T.op("dve", lambda e: e.tensor_copy(out=evb[i][:, :n], in_=PS[bk][:, :n]),
                                     [psB[bk]], [evbB[i]])
                                c0 = s - 16
                                T.dma("sp", kT_c[l][hd // 4][(hd % 4) * 128:(hd % 4 + 1) * 128, c0:c0 + n], evb[i][:, :n],
                                      reads=[evbB[i], kcB])
                            else:
                                T.op("dve", lambda e: e.tensor_copy(out=kT_ms[:, hd, 16:272], in_=PS[bk][:, :n]),
                                     [psB[bk]], [kmsB])
                    elif idx < 16:
                        ch = idx - 8
                        g = ch // 2
                        cc = ch % 2
                        for ti, t in enumerate(tiles):
                            s, n = tile_cols(t)
                            ls = s - t0
                            bk = 4 + ti if cc == 0 else 6 + ti
                            T.pe_group([(lambda e, k=k: e.matmul(PS[bk][:, :n], lhsT=wb[:, k, :], rhs=hn[:, k, ls:ls + n],
                                                                  start=(k == 0), stop=(k == 15)),
                                         [wB, hnB[k][ti]], [psB[bk]]) for k in range(16)])
                        if cc == 1:
                            for ti, t in enumerate(tiles):
                                pool_tile(l, g, t, [PS[4 + ti], PS[6 + ti]], [psB[4 + ti], psB[6 + ti]],
                                          (Eb, Ab, Bb, Es, As, Bs, Em, Am, Bm, dd, ddm), poolB,
                                          uhead, utail, umeta, uhtB, wp, wpB, o_pool, opB)
                    else:
                        hd = idx - 16
                        for ti, t in enumerate(tiles):
                            s, n = tile_cols(t)
                            ls = s - t0
                            if t == 0:
                                blocks = [(0, 16), (16, 128), (144, 128)]
                            elif t < 3:
                                blocks = [(0, 128), (128, 128), (256, 128)]
                            else:
                                blocks = [(0, 64), (64, 64), (128, 64), (192, 64)]
                            for bi, (b0, bn) in enumerate(blocks):
                                bk = pbank % 4; pbank += 1
                                i = bk % 2
                                T.pe_group([(lambda e, k=k: e.matmul(PS[bk][:bn, :128], lhsT=hn[:, k, ls + b0:ls + b0 + bn],
                                                                      rhs=wb[:, k, :], start=(k == 0), stop=(k == 15)),
                                             [wB, hnB[k][ti]], [psB[bk]]) for k in range(16)])
                                T.op("act", lambda e: e.activation(out=evf[i][:bn, :128], in_=PS[bk][:bn, :128], func=AF.Copy),
                                     [psB[bk]], [evfB[i]])
                                T.dma("sp", v_out[l, s + b0:s + b0 + bn, hd * 128:(hd + 1) * 128], evf[i][:bn, :128],
                                      reads=[evfB[i]])
                                if t == 0 and bi == 0:
                                    T.op("dve", lambda e: e.tensor_copy(out=v_ms[:16, 0, hd * 128:(hd + 1) * 128],
                                                                        in_=PS[bk][:16, :128]), [psB[bk]], [vmsB])
                                elif t == 3:
                                    T.op("dve", lambda e: e.tensor_copy(out=v_ms[:64, 1 + bi, hd * 128:(hd + 1) * 128],
                                                                        in_=PS[bk][:64, :128]), [psB[bk]], [vmsB])
                                else:
                                    T.op("dve", lambda e: e.tensor_copy(out=vtb[i][:, :], in_=PS[bk][:, :128]),
                                         [psB[bk]], [vtbB[i]])
                                    r0 = s + b0 - 16
                                    T.dma("sp", v_c[l][r0 // 512][r0 % 512:r0 % 512 + 128, hd * 128:(hd + 1) * 128], vtb[i][:, :],
                                          reads=[vtbB[i], vcB])
                    if idx + 2 < 24:
                        load_win(idx + 2, colof(idx + 2))
                T.barrier()
            T.dma("sp", ut_c[l].rearrange("p (b c t) -> p b c t", b=8, c=8), utail, reads=[uhtB, utcB])
            kgB, vgB, utgB = Buf(), Buf(), Buf()
            T.collective([ut_c[l]], [ut_G[l]], writes=[utcB, utgB])
            for i2 in range(2):
                T.collective([kT_c[l][i2]], [kT_G[l][i2]], writes=[kcB, kgB])
            for i2 in range(2):
                T.collective([v_c[l][i2]], [v_G[l][i2]], writes=[vcB, vgB])
            T.barrier()
            for p in range(2):
                tiles = [2 * p, 2 * p + 1]
                t0 = TT[tiles[0]][0]
                o = oS
                hn = bfv(o, NCH * 656).rearrange("p (c t) -> p c t", c=NCH); o += NCH * 656 // 2
                hnB = bufs(NCH, 2)
                win = []
                for i in range(2):
                    win.append(bfv(o, 16 * 128).rearrange("p (k c) -> p k c", k=16)); o += 1024
                winB = bufs(2)
                sq = [bfv(o, 384), bfv(o + 192, 384)]; o += 384
                sqB = bufs(2)
                lnv = f32v(o, 384); o += 384
                rstd = f32v(o, 656); o += 656
                scrB = Buf()
                assert o <= A_END, o

                def load_q(idx):
                    T.dma("pool", win[idx % 2],
                          w_in[l, :, idx * 128:(idx + 1) * 128].rearrange("(k p) c -> p k c", p=128), writes=[winB[idx % 2]])

                load_q(0)
                load_q(1)
                pre_norm(tiles, gi, hn, hnB, sq, sqB, lnv, rstd, scrB, 7)
                pbank = 0
                for hd in range(8):
                    for ti, t in enumerate(tiles):
                        s, n = tile_cols(t)
                        ls = s - t0
                        bk = pbank % 4; pbank += 1
                        T.pe_group([(lambda e, k=k: e.matmul(PS[bk][:, :n], lhsT=win[hd % 2][:, k, :], rhs=hn[:, k, ls:ls + n],
                                                              start=(k == 0), stop=(k == 15)),
                                     [winB[hd % 2], hnB[k][ti]], [psB[bk]]) for k in range(16)])
                        T.op("act", lambda e: e.activation(out=qT[:, hd, s:s + n], in_=PS[bk][:, :n], func=AF.Copy,
                                                           scale=float(128 ** -0.5)), [psB[bk]], [qB[hd][t]])
                    if hd + 2 < 8:
                        load_q(hd + 2)
                T.barrier()

            halo_fix(l, oS, uhead, umeta, uhtB, utgB, wp, wpB, o_pool, opB)
            T.barrier()
            sample_att(l, oS + 8 * NT // 2, qT, qB, kT_ms, kmsB, v_ms, vmsB, o_sb, osB, oUH)
            T.barrier()
            prompt_att(l, oS + 8 * NT // 2, qT, qB, kT_ms, kmsB, v_ms, vmsB, kgB, vgB, o_sb, osB, oUH)
            T.barrier()
            go = (l * 6 + 3) * 16
            for p in range(2):
                tiles = [2 * p, 2 * p + 1]
                t0 = TT[tiles[0]][0]
                o = A_PH + 8 * NT // 2
                y = f32v(o, NCH * 656).rearrange("p (c t) -> p c t", c=NCH); o += NCH * 656
                yB = bufs(NCH, 2)
                assert o <= oS
                o = oS + 8 * NT // 2
                wo = []
                for i in range(2):
                    wo.append(bfv(o, 16 * 128).rearrange("p (k c) -> p k c", k=16)); o += 1024
                woB = bufs(2)
                sq = [bfv(o, 384), bfv(o + 192, 384)]; o += 384
                sqB = bufs(2)
                lnv = f32v(o, 384); o += 384
                rstd = f32v(o, 656); o += 656
                tmp = [f32v(o, 384), f32v(o + 384, 384)]; o += 768
                tmpB = bufs(2)
                scrB = Buf()
                assert o <= A_END, o

                def load_o(c):
                    T.dma("pool", wo[c % 2],
                          w_out[l, :, c * 128:(c + 1) * 128].rearrange("(k p) c -> p k c", p=128), writes=[woB[c % 2]])

                load_o(0)
                load_o(1)
                for c in range(NCH):
                    for ti, t in enumerate(tiles):
                        s, n = tile_cols(t)
                        ls = s - t0
                        bk = (c % 2) * 2 + ti

                        def rhs_of(k):
                            return (o_sb[:, k, s:s + n], osB[k][t]) if k < 8 else (o_pool[:, k - 8, s:s + n], opB[k - 8][t])

                        T.pe_group([(lambda e, k=k: e.matmul(PS[bk][:, :n], lhsT=wo[c % 2][:, k, :], rhs=rhs_of(k)[0],
                                                              start=(k == 0), stop=(k == 15)),
                                     [woB[c % 2], rhs_of(k)[1]], [psB[bk]]) for k in range(16)])
                        i = (2 * c + ti) % 2
                        T.op("act", lambda e: e.activation(out=y[:, c, ls:ls + n], in_=PS[bk][:, :n], func=AF.Copy,
                                                           scale=gT[:, go + c:go + c + 1]), [psB[bk], cB], [yB[c][ti]])
                        T.op("act", lambda e: e.activation(out=sq[i][:, :n], in_=PS[bk][:, :n], func=AF.Square),
                             [psB[bk]], [sqB[i]])
                        T.pe_group([(lambda e: e.matmul(PS[4 + ti][:, :n], lhsT=ones_bf, rhs=sq[i][:, :n],
                                                        start=(c == 0), stop=(c == NCH - 1)),
                                     [sqB[i], cB], [psB[4 + ti]])])
                    if c + 2 < NCH:
                        load_o(c + 2)
                for ti, t in enumerate(tiles):
                    s, n = tile_cols(t)
                    rstd_from_ss(PS[4 + ti], psB[4 + ti], n, lnv, rstd[:, s - t0:s - t0 + n], False, scrB)
                post_res(tiles, t0, y, yB, rstd, scrB, tmp, tmpB)
                T.barrier()

        def pool_tile(l, g, t, pss, pssB, bfs, poolB, uhead, utail, umeta, uhtB, wp, wpB, o_pool, opB):
            Eb, Ab, Bb, Es, As, Bs, Em, Am, Bm, dd, ddm = bfs
            w = WINS[g]
            s, n = tile_cols(t)
            runs = []
            if t == 0:
                runs.append((Em, Am, Bm, 1, 16, 0, True))
                runs.append((Eb, Ab, Bb, 2, 128, 16, False))
            elif t < 3:
                runs.append((Eb, Ab, Bb, 3, 128, 0, False))
            else:
                runs.append((Es, As, Bs, 4, 64, 0, False))
            for (E, A, B, nr, ln, c0, is_meta) in runs:
                X = 16 + ln
                for cc in range(2):
                    src = pss[cc][:, c0:c0 + nr * ln].rearrange("p (r x) -> p r x", r=nr)
                    T.op("act", lambda e: e.activation(out=E[:, cc, :nr, 16:X], in_=src, func=AF.Copy),
                         [pssB[cc]], [poolB])
                    ch = 2 * g + cc
                    if t == 3:
                        T.dma("sp", E[:, cc, :4, 0:16], spT[l, :, ch, :, :].rearrange("r p x -> p r x"), writes=[poolB])
                        T.dma("sp", pool_out[l, 0:4, ch, :, :].rearrange("r p x -> p r x"), E[:, cc, :4, 64:80],
                              reads=[poolB])
                    elif is_meta:
                        T.op("dve", lambda e: e.tensor_copy(out=umeta[:, ch, :], in_=E[:, cc, 0, 16:32]),
                             [poolB], [uhtB])
                    else:
                        jb = (s + c0 - 16) // 128
                        T.op("dve", lambda e: e.tensor_copy(out=uhead[:, jb:jb + nr, ch, :], in_=E[:, cc, :nr, 16:32]),
                             [poolB], [uhtB])
                        T.op("dve", lambda e: e.tensor_copy(out=utail[:, jb:jb + nr, ch, :], in_=E[:, cc, :nr, X - 16:X]),
                             [poolB], [uhtB])
                        if t == 2:
                            T.dma("sp", pool_out[l, 4, ch, :, :], E[:, cc, 2, X - 16:X], reads=[poolB])
                src, dst, sh = E, A, 1
                for step in range(g + 1):
                    lo = 2 * sh - 1
                    for cc in range(2):
                        T.op("dve", lambda e, src=src, dst=dst, sh=sh, lo=lo: e.tensor_tensor(
                            out=dst[:, cc, :nr, lo:X], in0=src[:, cc, :nr, lo:X], in1=src[:, cc, :nr, lo - sh:X - sh],
                            op=ALU.add), [poolB], [poolB])
                    src, dst = dst, (B if dst is A else A)
                    sh *= 2
                sw = src
                for cc in range(2):
                    if is_meta:
                        rc = cst[:, C_RCM + 16 * g:C_RCM + 16 * g + 16]
                        T.op("dve", lambda e: e.tensor_tensor(out=A[:, cc, 0, 0:16], in0=sw[:, cc, 0, 16:32], in1=rc,
                                                              op=ALU.mult), [poolB, cB], [poolB])
                        T.op("dve", lambda e: e.tensor_tensor(out=ddm[:, cc, :], in0=A[:, cc, 0, 0:16],
                                                              in1=E[:, cc, 0, 16:32], op=ALU.subtract), [poolB], [poolB])
                    else:
                        dv = dd[:, cc, 0:nr * ln].rearrange("p (r x) -> p r x", r=nr)
                        T.op("dve", lambda e: e.scalar_tensor_tensor(
                            out=dv, in0=sw[:, cc, :nr, 16:X], scalar=1.0 / w, in1=E[:, cc, :nr, 16:X],
                            op0=ALU.mult, op1=ALU.subtract), [poolB], [poolB])
                ncol = nr * ln
                for dc in range(2):
                    bk = 2 + dc
                    if is_meta:
                        rhs = [ddm[:, 0, :], ddm[:, 1, :]]
                    else:
                        rhs = [dd[:, 0, 0:ncol], dd[:, 1, 0:ncol]]
                    T.pe_group([(lambda e, cc=cc: e.matmul(PS[bk][:, :ncol], lhsT=wp[:, g, cc, dc * 128:(dc + 1) * 128],
                                                            rhs=rhs[cc], start=(cc == 0), stop=(cc == 1)),
                                 [wpB, poolB], [psB[bk]]) for cc in range(2)])
                    ch = 2 * g + dc
                    T.op("act", lambda e: e.activation(out=o_pool[:, ch, s + c0:s + c0 + ncol], in_=PS[bk][:, :ncol],
                                                       func=AF.Copy, scale=psT[:, l * 8 + ch:l * 8 + ch + 1]),
                         [psB[bk], cB], [opB[ch][t]])

        def halo_fix(l, o, uhead, umeta, uhtB, utgB, wp, wpB, o_pool, opB):
            cand = f32v(o, 4 * 1024).rearrange("p (v b c t) -> p v b c t", v=4, b=8, c=8); o += 4096
            candB = Buf()
            EW = 8 * 8 * 32
            Sx = []
            for i in range(3):
                Sx.append(f32v(o, EW).rearrange("p (b c x) -> p b c x", b=8, c=8)); o += EW
            dh = bfv(o, 8 * 8 * 16).rearrange("p (c b x) -> p c b x", c=8, b=8); o += 512
            fxB = Buf()
            assert o <= A_END, o
            G = ut_G[l]
            T.dma("sp", cand[:, 0, 1:8, :, :], G[384:512, 0:896].rearrange("p (b c t) -> p b c t", b=7, c=8),
                  reads=[utgB], writes=[candB])
            T.op("dve", lambda e: e.tensor_copy(out=cand[:, 0, 0, :, :], in_=umeta), [uhtB], [candB])
            for v in range(1, 4):
                T.dma("sp", cand[:, v, :, :, :], G[(v - 1) * 128:v * 128, :].rearrange("p (b c t) -> p b c t", b=8, c=8),
                      reads=[utgB], writes=[candB])
            E, A, B = Sx
            for i in range(3):
                T.op("dve", lambda e, i=i: e.memset(Sx[i], 0.0), [], [fxB])
            for b in range(8):
                T.op("dve", lambda e: e.tensor_scalar(out=E[:, b, :, 0:16], in0=cand[:, 0, b, :, :],
                                                      scalar1=cst[:, C_SEL:C_SEL + 1], scalar2=None, op0=ALU.mult),
                     [candB, cB], [fxB])
                for v in range(1, 4):
                    T.op("dve", lambda e, v=v: e.scalar_tensor_tensor(
                        out=E[:, b, :, 0:16], in0=cand[:, v, b, :, :], scalar=cst[:, C_SEL + v:C_SEL + v + 1],
                        in1=E[:, b, :, 0:16], op0=ALU.mult, op1=ALU.add), [candB, cB, fxB], [fxB])
                T.op("dve", lambda e: e.tensor_copy(out=E[:, b, :, 16:32], in_=uhead[:, b, :, :]), [uhtB], [fxB])
            src_, dst_, sh = E, A, 1
            for g in range(4):
                lo = 2 * sh - 1
                for b in range(8):
                    T.op("dve", lambda e, b=b: e.tensor_tensor(
                        out=dst_[:, b, :, lo:32], in0=src_[:, b, :, lo:32], in1=src_[:, b, :, lo - sh:32 - sh], op=ALU.add),
                        [fxB], [fxB])
                sw = dst_
                for cc in range(2):
                    ch = 2 * g + cc
                    T.op("dve", lambda e: e.scalar_tensor_tensor(
                        out=dh[:, ch, :, :], in0=sw[:, :, ch, 16:32], scalar=1.0 / WINS[g], in1=E[:, :, ch, 16:32],
                        op0=ALU.mult, op1=ALU.subtract), [fxB], [fxB])
                src_, dst_ = dst_, (B if dst_ is A else A)
                sh *= 2
            for g in range(4):
                for dc in range(2):
                    bk = 2 + dc
                    T.pe_group([(lambda e, cc=cc: e.matmul(PS[bk][:, :128], lhsT=wp[:, g, cc, dc * 128:(dc + 1) * 128],
                                                            rhs=dh[:, 2 * g + cc, :, :].rearrange("p b x -> p (b x)"),
                                                            start=(cc == 0), stop=(cc == 1)),
                                 [wpB, fxB], [psB[bk]]) for cc in range(2)])
                    ch = 2 * g + dc
                    dst = o_pool[:, ch, 16:16 + 1024].rearrange("p (b x) -> p b x", b=8)[:, :, 0:16]
                    T.op("act", lambda e: e.activation(out=dst, in_=PS[bk][:, :128].rearrange("p (b x) -> p b x", b=8),
                                                       func=AF.Copy, scale=psT[:, l * 8 + ch:l * 8 + ch + 1]),
                         [psB[bk], cB], [opB[ch][0], opB[ch][1], opB[ch][2]])


        def sample_att(l, o, qT, qB, kT_ms, kmsB, v_ms, vmsB, o_sb, osB, xbuf):
            kc = []
            for i in range(2):
                kc.append(bfv(o, 8 * 512).rearrange("p (h k) -> p h k", h=8)); o += 2048
            kcB_ = bufs(2)
            vc = []
            for i in range(2):
                vc.append(bfv(o, 2 * 1024).rearrange("p (t d) -> p t d", t=2)); o += 1024
            vcB_ = bufs(2)
            e_sb = f32v(o, 512); o += 512
            sp2 = [bfv(o, 512), bfv(xbuf, 512)]; o += 256
            a2 = [bfv(o, 512), bfv(xbuf + 256, 512)]; o += 256
            S = bfv(o, 512); o += 256
            eB, SB = Buf(), Buf()
            spB, aB = bufs(2), bufs(2)
            assert o <= A_END, o

            def load_k(s_, grp):
                i = (s_ * 8 + grp) % 2
                T.dma("pool", kc[i], ckT[l, s_, :, :, grp * 512:(grp + 1) * 512].rearrange("h d k -> d h k"),
                      writes=[kcB_[i]])

            def load_v(s_, pr):
                i = (s_ * 16 + pr) % 2
                T.dma("pool", vc[i], cv[l, s_, pr * 256:(pr + 1) * 256, :].rearrange("(t p) d -> p t d", p=128),
                      writes=[vcB_[i]])

            OB = 4
            NKT = PAST // 128
            TS = [-1] + list(range(NKT - 1, -1, -1))
            NTI = len(TS)
            for s_ in range(4):
                q0 = 1040 + 64 * s_
                tq = 3
                T.op("dve", lambda e: e.memset(S, 0.0), [], [SB])
                T.pe_group([(lambda e: e.matmul(PS[OB][:, :], lhsT=zeros_bf[:, 0:128], rhs=zeros_bf, start=True, stop=False),
                             [cB], [psB[OB]])])
                load_k(s_, PAST // 512 - 1)
                load_v(s_, PAST // 256 - 1)

                def stage_Z(i):
                    t = TS[i]
                    pb = i % 2
                    P = PS[pb]
                    if t >= 0:
                        grp = t // 4
                        if t % 4 == 3 and grp > 0:
                            load_k(s_, grp - 1)
                        ki = (s_ * 8 + grp) % 2
                        kr = 128
                        items = [(lambda e, hd=hd: e.matmul(P[:, hd * 64:(hd + 1) * 64],
                                                            lhsT=kc[ki][:, hd, (t % 4) * 128:(t % 4) * 128 + 128],
                                                            rhs=qT[:, hd, q0:q0 + 64], start=(hd == 0), stop=False),
                                  [kcB_[ki], qB[hd][tq]], [psB[pb]]) for hd in range(8)]
                    else:
                        kr = 64
                        items = [(lambda e, hd=hd: e.matmul(P[:64, hd * 64:(hd + 1) * 64],
                                                            lhsT=kT_ms[:, hd, 16 + 64 * s_:16 + 64 * s_ + 64],
                                                            rhs=qT[:, hd, q0:q0 + 64], start=(hd == 0), stop=False),
                                  [kmsB, qB[hd][tq]], [psB[pb]]) for hd in range(8)]
                        items += [(lambda e, hd=hd: e.matmul(P[:64, hd * 64:(hd + 1) * 64], lhsT=ident_bf[:64, :64],
                                                             rhs=maskb_bf[:64, 4, 0:64], start=False, stop=False),
                                   [cB], [psB[pb]]) for hd in range(8)]
                    T.pe_group(items)
                    T.op("act", lambda e: e.activation(out=e_sb[:kr, :], in_=P[:kr, :], func=AF.Exp), [psB[pb]], [eB])
                    T.op("act", lambda e: e.activation(out=sp2[i % 2][:kr, :], in_=e_sb[:kr, :], func=AF.Ln, bias=1.0),
                         [eB], [spB[i % 2]])

                def stage_T(i):
                    t = TS[i]
                    pb = i % 2
                    P = PS[pb]
                    kr = 128 if t >= 0 else 64
                    items = [(lambda e: e.matmul(P[:kr, :], lhsT=negtri_bf[:kr, :kr], rhs=sp2[i % 2][:kr, :], start=False,
                                                 stop=(t == -1)), [spB[i % 2], cB], [psB[pb]])]
                    if t != -1:
                        items.append((lambda e: e.matmul(P[:kr, :], lhsT=negones_bf[:, :kr], rhs=S[:, :], start=False,
                                                         stop=True), [SB, cB], [psB[pb]]))
                    T.pe_group(items)

                def stage_E2(i):
                    t = TS[i]
                    pb = i % 2
                    kr = 128 if t >= 0 else 64
                    T.op("act", lambda e: e.activation(out=a2[i % 2][:kr, :], in_=PS[pb][:kr, :], func=AF.Exp),
                         [psB[pb]], [aB[i % 2]])
                    T.op("dve", lambda e: e.tensor_tensor(out=S[:kr, :], in0=S[:kr, :], in1=sp2[i % 2][:kr, :], op=ALU.add),
                         [spB[i % 2], SB], [SB])

                def stage_AV(i):
                    t = TS[i]
                    a_sb = a2[i % 2]
                    if t >= 0:
                        pr = t // 2
                        if t % 2 == 1 and pr > 0:
                            load_v(s_, pr - 1)
                        vi = (s_ * 16 + pr) % 2
                        items = [(lambda e, hd=hd: e.matmul(PS[OB][:, hd * 64:(hd + 1) * 64],
                                                            lhsT=vc[vi][:, t % 2, hd * 128:(hd + 1) * 128],
                                                            rhs=a_sb[:, hd * 64:(hd + 1) * 64], start=False, stop=(t == 0)),
                                  [vcB_[vi], aB[i % 2]], [psB[OB]]) for hd in range(8)]
                    else:
                        items = [(lambda e, hd=hd: e.matmul(PS[OB][:, hd * 64:(hd + 1) * 64],
                                                            lhsT=v_ms[:64, 1 + s_, hd * 128:(hd + 1) * 128],
                                                            rhs=a_sb[:64, hd * 64:(hd + 1) * 64], start=False, stop=False),
                                  [vmsB, aB[i % 2]], [psB[OB]]) for hd in range(8)]
                    T.pe_group(items)

                for i in range(-1, NTI):
                    if i + 1 < NTI:
                        stage_Z(i + 1)
                    if i >= 0:
                        stage_T(i)
                    if i >= 1:
                        stage_AV(i - 1)
                    if i >= 0:
                        stage_E2(i)
                stage_AV(NTI - 1)
                T.op("act", lambda e: e.activation(out=o_sb[:, :, q0:q0 + 64],
                                                   in_=PS[OB][:, :].rearrange("p (h q) -> p h q", h=8), func=AF.Copy),
                     [psB[OB]], [osB[hd][3] for hd in range(8)])

        def prompt_att(l, o, qT, qB, kT_ms, kmsB, v_ms, vmsB, kgB, vgB, o_sb, osB, xbuf):
            kTh = bfv(o, 16 + 4096); o += (16 + 4096) // 2
            kThB = Buf()
            Vh = bfv(o, 33 * 128).rearrange("p (t d) -> p t d", t=33); o += 33 * 64
            VhB = Buf()
            e_sb = f32v(o, 1024); o += 1024
            sp2 = [bfv(o, 1024), bfv(xbuf, 1024)]; o += 512
            a2 = [bfv(o, 1024), bfv(xbuf + 512, 1024)]; o += 512
            S = bfv(o, 1024); o += 512
            em = f32v(o, 16); o += 16
            spm = bfv(o, 16); o += 8
            am = bfv(o, 16); o += 8
            eB, SB, mB = Buf(), Buf(), Buf()
            spB, aB = bufs(2), bufs(2)
            assert o <= A_END, o
            KG, VG = kT_G[l], v_G[l]
            OBK = [4, 5]
            OM = 6
            TS = list(range(31, -1, -1)) + [-1]
            NTI = len(TS)

            def segs_of(t):
                if t < 0:
                    return [(0, 0, 512), (1, 0, 512)]
                jmin = t // 4
                sg = []
                for half in range(2):
                    lo = max(jmin * 128, half * 512)
                    hi = (half + 1) * 512
                    if lo < hi:
                        sg.append((half, lo - half * 512, hi - half * 512))
                return sg

            for hd in range(8):
                for r in range(4):
                    T.dma("sp", kTh[:, 16 + r * 1024:16 + (r + 1) * 1024],
                          KG[hd // 4][r * 512 + (hd % 4) * 128:r * 512 + (hd % 4 + 1) * 128, :], reads=[kgB], writes=[kThB])
                    for i2 in range(2):
                        T.dma("sp", Vh[:, 1 + r * 8 + 4 * i2:5 + r * 8 + 4 * i2, :],
                              VG[i2][r * 512:(r + 1) * 512, hd * 128:(hd + 1) * 128].rearrange("(j p) d -> p j d", p=128),
                              reads=[vgB], writes=[VhB])
                T.op("dve", lambda e: e.tensor_copy(out=kTh[:, 0:16], in_=kT_ms[:, hd, 0:16]), [kmsB], [kThB])
                T.op("dve", lambda e: e.tensor_copy(out=Vh[:16, 0, :], in_=v_ms[:16, 0, hd * 128:(hd + 1) * 128]),
                     [vmsB], [VhB])
                T.op("dve", lambda e: e.memset(S, 0.0), [], [SB])
                for bk in OBK:
                    T.pe_group([(lambda e, bk=bk: e.matmul(PS[bk][:, :], lhsT=zeros_bf[:, 0:128], rhs=zeros_bf,
                                                            start=True, stop=False), [cB], [psB[bk]])])
                qrd = [qB[hd][0], qB[hd][1], qB[hd][2]]

                def stage_Z(i):
                    t = TS[i]
                    pset = (i % 2) * 2
                    kr = 128 if t >= 0 else 16
                    kcol = 16 + (t % 4) * 1024 + (t // 4) * 128 if t >= 0 else 0
                    items = []
                    for (half, lo, hi) in segs_of(t):
                        bk = pset + half
                        items.append((lambda e, bk=bk, half=half, lo=lo, hi=hi: e.matmul(
                            PS[bk][:kr, lo:hi], lhsT=kTh[:, kcol:kcol + kr],
                            rhs=qT[:, hd, 16 + half * 512 + lo:16 + half * 512 + hi], start=True, stop=False),
                            [kThB] + qrd, [psB[bk]]))
                    if t >= 0:
                        jmin = t // 4
                        half = jmin // 4
                        x0 = (jmin % 4) * 128
                        items.append((lambda e: e.matmul(PS[pset + half][:, x0:x0 + 128], lhsT=ident_bf,
                                                         rhs=maskb_bf[:, t % 4, :], start=False, stop=False),
                                      [cB], [psB[pset + half]]))
                    T.pe_group(items)
                    sp_sb = sp2[i % 2]
                    for (half, lo, hi) in segs_of(t):
                        bk = pset + half
                        g0 = half * 512
                        T.op("act", lambda e: e.activation(out=e_sb[:kr, g0 + lo:g0 + hi], in_=PS[bk][:kr, lo:hi],
                                                           func=AF.Exp), [psB[bk]], [eB])
                        T.op("act", lambda e: e.activation(out=sp_sb[:kr, g0 + lo:g0 + hi], in_=e_sb[:kr, g0 + lo:g0 + hi],
                                                           func=AF.Ln, bias=1.0), [eB], [spB[i % 2]])

                def stage_T(i):
                    t = TS[i]
                    pset = (i % 2) * 2
                    kr = 128 if t >= 0 else 16
                    sp_sb = sp2[i % 2]
                    items = []
                    for (half, lo, hi) in segs_of(t):
                        bk = pset + half
                        g0 = half * 512
                        items.append((lambda e, bk=bk, lo=lo, hi=hi, g0=g0: e.matmul(
                            PS[bk][:kr, lo:hi], lhsT=negtri_bf[:kr, :kr], rhs=sp_sb[:kr, g0 + lo:g0 + hi],
                            start=False, stop=False), [spB[i % 2], cB], [psB[bk]]))
                        items.append((lambda e, bk=bk, lo=lo, hi=hi, g0=g0: e.matmul(
                            PS[bk][:kr, lo:hi], lhsT=negones_bf[:, :kr], rhs=S[:, g0 + lo:g0 + hi],
                            start=False, stop=True), [SB, cB], [psB[bk]]))
                    T.pe_group(items)

                def stage_E2(i):
                    t = TS[i]
                    pset = (i % 2) * 2
                    kr = 128 if t >= 0 else 16
                    a_sb = a2[i % 2]
                    for (half, lo, hi) in segs_of(t):
                        bk = pset + half
                        g0 = half * 512
                        T.op("act", lambda e: e.activation(out=a_sb[:kr, g0 + lo:g0 + hi], in_=PS[bk][:kr, lo:hi],
                                                           func=AF.Exp), [psB[bk]], [aB[i % 2]])
                    if t >= 0:
                        qlo = (t // 4) * 128
                        sp_sb = sp2[i % 2]
                        T.op("dve", lambda e: e.tensor_tensor(out=S[:, qlo:1024], in0=S[:, qlo:1024],
                                                              in1=sp_sb[:, qlo:1024], op=ALU.add),
                             [spB[i % 2], SB], [SB])

                def stage_AV(i):
                    t = TS[i]
                    kr = 128 if t >= 0 else 16
                    a_sb = a2[i % 2]
                    vt = 1 + (t % 4) * 8 + t // 4 if t >= 0 else 0
                    items = []
                    for (half, lo, hi) in segs_of(t):
                        g0 = half * 512
                        items.append((lambda e, half=half, lo=lo, hi=hi, g0=g0: e.matmul(
                            PS[OBK[half]][:, lo:hi], lhsT=Vh[:kr, vt, :], rhs=a_sb[:kr, g0 + lo:g0 + hi],
                            start=False, stop=(t < 0)), [VhB, aB[i % 2]], [psB[OBK[half]]]))
                    T.pe_group(items)

                for i in range(-1, NTI):
                    if i + 1 < NTI:
                        stage_Z(i + 1)
                    if i >= 0:
                        stage_T(i)
                    if i >= 1:
                        stage_AV(i - 1)
                    if i >= 0:
                        stage_E2(i)
                stage_AV(NTI - 1)
                T.pe_group([(lambda e: e.matmul(PS[OM][:16, 0:16], lhsT=kTh[:, 0:16], rhs=qT[:, hd, 0:16], start=True, stop=False),
                             [kThB, qB[hd][0]], [psB[OM]]),
                            (lambda e: e.matmul(PS[OM][:16, 0:16], lhsT=ident_bf[:16, :16], rhs=maskb_bf[:16, 4, 0:16],
                                                start=False, stop=False), [cB], [psB[OM]])])
                T.op("act", lambda e: e.activation(out=em[:16, :], in_=PS[OM][:16, 0:16], func=AF.Exp), [psB[OM]], [mB])
                T.op("act", lambda e: e.activation(out=spm[:16, :], in_=em[:16, :], func=AF.Ln, bias=1.0), [mB], [mB])
                T.pe_group([(lambda e: e.matmul(PS[OM][:16, 0:16], lhsT=negtri_bf[:16, :16], rhs=spm[:16, :], start=False, stop=True),
                             [mB, cB], [psB[OM]])])
                T.op("act", lambda e: e.activation(out=am[:16, :], in_=PS[OM][:16, 0:16], func=AF.Exp), [psB[OM]], [mB])
                T.pe_group([(lambda e: e.matmul(PS[OM][:, 16:32], lhsT=Vh[:16, 0, :], rhs=am[:16, :], start=True, stop=True),
                             [VhB, mB], [psB[OM]])])
                T.op("act", lambda e: e.activation(out=o_sb[:, hd, 0:16], in_=PS[OM][:, 16:32], func=AF.Copy),
                     [psB[OM]], [osB[hd][0]])
                T.op("act", lambda e: e.activation(out=o_sb[:, hd, 16:528], in_=PS[OBK[0]][:, :], func=AF.Copy),
                     [psB[OBK[0]]], [osB[hd][0], osB[hd][1]])
                T.op("dve", lambda e: e.tensor_copy(out=o_sb[:, hd, 528:1040], in_=PS[OBK[1]][:, :]),
                     [psB[OBK[1]]], [osB[hd][1], osB[hd][2]])

        for l in range(depth):
            if "ffn1" in phases:
                ffn(l, 0)
            if "mix" in phases:
                mixer(l)
            if "ffn2" in phases:
                ffn(l, 1)
        for c in range(NCH):
            T.dma("sp", yT[c, :, :], h[:, c, :], reads=hB[c])
        T.barrier()
    return nc


def _consts(c):
    cst = np.zeros((128, 1024), np.float32)
    cst[:, C_ID:C_ID + 128] = np.eye(128, dtype=np.float32)
    j = np.arange(128)[:, None]
    k = np.arange(128)[None, :]
    cst[:, C_NTRI:C_NTRI + 128] = np.where(j >= k, -1.0, 0.0)
    tri = np.where(j < k, 0.0, NEG).astype(np.float32)
    for v in range(4):
        if v < c:
            m = np.zeros((128, 128), np.float32)
        elif v == c:
            m = tri
        else:
            m = np.full((128, 128), NEG, np.float32)
        cst[:, C_MASK + v * 128:C_MASK + (v + 1) * 128] = m
    cst[:, C_MASK + 512:C_MASK + 640] = tri
    cst[:, C_SEL + c] = 1.0
    for g, w in enumerate(WINS):
        cst[:, C_RCM + 16 * g:C_RCM + 16 * g + 16] = 1.0 / np.minimum(float(w), np.arange(16) + 1.0)
    return cst


_NC_CACHE = {}


def kernel(x_prompt, x_sample, cache_k, cache_v, state_pool, meta_tokens,
           w_in, w_out, w_pool, pool_scale, norm_gains,
           ffn1_gate, ffn1_up, ffn1_down, ffn2_gate, ffn2_up, ffn2_down, _depth=L,
           _phases=("ffn1", "mix", "ffn2"), _past=PAST, _ncores=N_CORES):
    f = np.float32
    x_prompt = np.asarray(x_prompt, f); x_sample = np.asarray(x_sample, f)
    cache_k = np.asarray(cache_k, f); cache_v = np.asarray(cache_v, f)
    state_pool = np.asarray(state_pool, f); meta_tokens = np.asarray(meta_tokens, f)
    shared = {"w_in": np.ascontiguousarray(w_in, f), "w_out": np.ascontiguousarray(w_out, f),
              "w_pool": np.ascontiguousarray(w_pool, f)}
    if "ffn1" in _phases:
        shared.update({"fg1": np.ascontiguousarray(ffn1_gate, f), "fu1": np.ascontiguousarray(ffn1_up, f),
                       "fd1": np.ascontiguousarray(ffn1_down, f)})
    if "ffn2" in _phases:
        shared.update({"fg2": np.ascontiguousarray(ffn2_gate, f), "fu2": np.ascontiguousarray(ffn2_up, f),
                       "fd2": np.ascontiguousarray(ffn2_down, f)})
    ng = np.asarray(norm_gains, f)
    gT = np.ascontiguousarray(ng.reshape(L, 6, NCH, 128).transpose(3, 0, 1, 2).reshape(128, L * 96))
    ps = np.asarray(pool_scale, f)
    psT = np.ascontiguousarray(ps.reshape(L, 8, 128).transpose(2, 0, 1).reshape(128, L * 8))
    shared["gT"] = gT
    shared["psT"] = psT
    in_maps = []
    for r in range(N_CORES):
        b, c = r // 4, r % 4
        cols = [meta_tokens]
        for j in range(8):
            g = 4 * j + c
            cols.append(x_prompt[b, 128 * g:128 * (g + 1)])
        for s in range(4):
            cols.append(x_sample[4 * r + s])
        xl = np.concatenate(cols, axis=0)
        xT = np.ascontiguousarray(xl.T.reshape(NCH, 128, NT))
        ck = cache_k[:, 4 * r:4 * r + 4, PAST - _past:]
        ckT = np.ascontiguousarray(ck.transpose(0, 1, 3, 4, 2))
        cvv = np.ascontiguousarray(cache_v[:, 4 * r:4 * r + 4, PAST - _past:].reshape(L, 4, _past, 1024))
        sp = state_pool[:, 4 * r:4 * r + 4]
        spT = np.zeros((L, 4, 8, 128, 16), f)
        spT[..., 1:] = sp.reshape(L, 4, 15, 8, 128).transpose(0, 1, 3, 4, 2)
        m = {"xT": xT, "ckT": ckT, "cv": cvv, "spT": spT, "cst": _consts(c)}
        m.update(shared)
        in_maps.append(m)
    key = (_depth, tuple(_phases), _past)
    if key not in _NC_CACHE:
        _NC_CACHE[key] = build(_depth, tuple(_phases), _past)
    nc = _NC_CACHE[key]
    res = run_bass_kernel_spmd(nc, in_maps[:_ncores], core_ids=list(range(_ncores)))
    R = list(res.results) + [res.results[0]] * (N_CORES - _ncores)
    y_prompt = np.zeros((2, 4096, DM), f)
    y_sample = np.zeros((32, 64, DM), f)
    nkp = np.zeros((L, 2, 16 + 4096, NH, 128), f)
    nvp = np.zeros((L, 2, 16 + 4096, NH, 128), f)
    npp = np.zeros((L, 2, 15, 1024), f)
    nks = np.zeros((L, 32, 64, NH, 128), f)
    nvs = np.zeros((L, 32, 64, NH, 128), f)
    nps = np.zeros((L, 32, 15, 1024), f)
    for r in range(N_CORES):
        b, c = r // 4, r % 4
        yl = R[r]["yT"].reshape(DM, NT).T
        kl = R[r]["kT_out"].reshape(L, NH * 128, NT).transpose(0, 2, 1).reshape(L, NT, NH, 128)
        vl = R[r]["v_out"].reshape(L, NT, NH, 128)
        pl = R[r]["pool_out"].reshape(L, 5, 1024, 16).transpose(0, 1, 3, 2)
        for j in range(8):
            g = 4 * j + c
            y_prompt[b, 128 * g:128 * (g + 1)] = yl[16 + 128 * j:16 + 128 * (j + 1)]
            nkp[:, b, 16 + 128 * g:16 + 128 * (g + 1)] = kl[:, 16 + 128 * j:16 + 128 * (j + 1)]
            nvp[:, b, 16 + 128 * g:16 + 128 * (g + 1)] = vl[:, 16 + 128 * j:16 + 128 * (j + 1)]
        if c == 0:
            nkp[:, b, 0:16] = kl[:, 0:16]
            nvp[:, b, 0:16] = vl[:, 0:16]
        if c == 3:
            npp[:, b] = pl[:, 4, 1:16]
        for s in range(4):
            y_sample[4 * r + s] = yl[1040 + 64 * s:1040 + 64 * (s + 1)]
            nks[:, 4 * r + s] = kl[:, 1040 + 64 * s:1040 + 64 * (s + 1)]
            nvs[:, 4 * r + s] = vl[:, 1040 + 64 * s:1040 + 64 * (s + 1)]
            nps[:, 4 * r + s] = pl[:, s, 1:16]
    return (y_prompt, y_sample, nkp, nvp, npp, nks, nvs, nps)
```
